# Optimizing a Trainium2 kernel written in Bass

```python
import math
import jax
import jax.numpy as jnp
from jax import lax
import numpy as np


D_MODEL = 2048
BATCH = 4
SEQ = 2048
DEPTH = 4
DEC_BATCH = 4
DEC_SEQ = 4096
PAST_LEN = 128

GRID_W = 64
CHUNK = 128
Q_BLOCK = 128
N_GROUPS = 4
D_MIX = D_MODEL
W_GROUP = D_MIX // N_GROUPS
A_HEADS = 4
A_HEAD_W = W_GROUP // A_HEADS
B_W = W_GROUP
HYENA_ORDER = 2
POS_BANDS = 16
POS_EMB = 2 * POS_BANDS + 1
FILT_H = 64
HYENA_TARGET = 1e-2
HYENA_DECAY_SHORT = 0.3
HYENA_DECAY_LONG = 1.5
C_HEADS = 4
C_KV_HEADS = 2
C_HD = W_GROUP // C_HEADS
D_HEADS = 4
Q_LORA = D_MODEL // 4
KV_LORA = D_MODEL // 8
NOPE = 128
ROPE_D = 64
V_HD = W_GROUP // D_HEADS
ROPE_THETA = 10000.0
D_FF = 11 * D_MODEL // 4
EPS = 1e-6
IN_A = 2 * W_GROUP
IN_B = 3 * B_W
IN_C = (C_HEADS + 2 * C_KV_HEADS) * C_HD
IN_D = Q_LORA + KV_LORA + ROPE_D
IN_COLS = IN_A + IN_B + IN_C + IN_D

kernel_name = 'hybrid_bidir_encoder_two_groups'


def rms_norm(x, g):
    xf = x.astype(jnp.float32)
    y = xf * lax.rsqrt(jnp.mean(xf * xf, axis=-1, keepdims=True) + EPS)
    return y.astype(x.dtype) * g


def layer_norm(x, g):
    xf = x.astype(jnp.float32)
    xc = xf - jnp.mean(xf, axis=-1, keepdims=True)
    y = xc * lax.rsqrt(jnp.mean(xc * xc, axis=-1, keepdims=True) + EPS)
    return y.astype(x.dtype) * g


def dwconv3(x, w, b):
    xp = jnp.pad(x, ((0, 0), (1, 1), (0, 0)))
    return xp[:, :-2] * w[0] + xp[:, 1:-1] * w[1] + xp[:, 2:] * w[2] + b


def rope_half_split(x, ang):
    cos = jnp.cos(ang)[:, None, :].astype(x.dtype)
    sin = jnp.sin(ang)[:, None, :].astype(x.dtype)
    x1, x2 = jnp.split(x, 2, axis=-1)
    return jnp.concatenate([x1 * cos - x2 * sin, x2 * cos + x1 * sin], axis=-1)


def axial_rope(x, row, col):
    sec = x.shape[-1] // 2
    inv = ROPE_THETA ** (-jnp.arange(0, sec, 2, dtype=jnp.float32) / sec)
    return jnp.concatenate([rope_half_split(x[..., :sec], row[:, None] * inv),
                            rope_half_split(x[..., sec:], col[:, None] * inv)], axis=-1)


def block_attention(q, k, v):
    b, s, hkv, g, dk = q.shape
    nb = s // Q_BLOCK
    scale = 1.0 / math.sqrt(dk)
    qb = jnp.moveaxis(q.reshape(b, nb, Q_BLOCK, hkv, g, dk), 1, 0)

    def one_block(qi):
        sc = jnp.einsum('bqhgd,bkhd->bhgqk', qi, k, preferred_element_type=jnp.float32) * scale
        p = jax.nn.softmax(sc, axis=-1).astype(v.dtype)
        return jnp.einsum('bhgqk,bkhe->bqhge', p, v)

    o = lax.map(one_block, qb)
    return jnp.moveaxis(o, 0, 1).reshape(b, s, hkv * g * v.shape[-1])


def gmlp_mixer(proj, vnorm_g, ws, bs):
    z = jax.nn.gelu(proj, approximate=False)
    u, v = jnp.split(z, 2, axis=-1)
    v = layer_norm(v, vnorm_g)
    b, s, _ = v.shape
    vc = v.reshape(b, s // CHUNK, CHUNK, A_HEADS, A_HEAD_W)
    mixed = jnp.einsum('hpq,bnqhc->bnphc', ws, vc) + bs.T[None, None, :, :, None]
    return u * mixed.reshape(b, s, W_GROUP)


def hyena_filters(L, w1, b1, f1, w2, b2, f2, w3, decay):
    pos = jnp.arange(L, dtype=jnp.float32)
    t = jnp.linspace(0.0, 1.0, L, dtype=jnp.float32)
    bands = jnp.linspace(1e-4, POS_BANDS - 1, POS_BANDS, dtype=jnp.float32)
    ang = (2.0 * math.pi / L) * pos[:, None] * bands[None, :]
    feats = jnp.concatenate([t[:, None], jnp.cos(ang), -jnp.sin(ang)], axis=-1)
    h = jnp.sin(f1 * (feats @ w1 + b1))
    h = jnp.sin(f2 * (h @ w2 + b2))
    h = h @ w3
    window = jnp.exp(-t[:, None] * jnp.abs(decay).astype(jnp.float32))
    return (h * window).astype(jnp.float32).reshape(L, HYENA_ORDER, 2, B_W)


def bidir_long_conv(u, h_fwd, h_bwd, skip):
    L, ch = h_fwd.shape
    k_circ = jnp.concatenate([h_fwd, jnp.zeros((1, ch), jnp.float32), h_bwd[:0:-1]], axis=0)
    kf = jnp.fft.rfft(k_circ, n=2 * L, axis=0)
    uf32 = u.astype(jnp.float32)
    uf = jnp.fft.rfft(uf32, n=2 * L, axis=1)
    y = jnp.fft.irfft(uf * kf[None], n=2 * L, axis=1)[:, :L]
    return (y + uf32 * skip.astype(jnp.float32)).astype(u.dtype)


def hyena_mixer(proj, conv_w, conv_b, w1, b1, f1, w2, b2, f2, w3, decay, skip):
    L = proj.shape[1]
    z = dwconv3(proj, conv_w, conv_b)
    x1, x2, v = jnp.split(z, 3, axis=-1)
    h = hyena_filters(L, w1, b1, f1, w2, b2, f2, w3, decay)
    y = x1 * bidir_long_conv(v, h[:, 0, 0], h[:, 0, 1], skip[0])
    y = x2 * bidir_long_conv(y, h[:, 1, 0], h[:, 1, 1], skip[1])
    return y


def gqa_mixer(proj, qn_g, kn_g, row, col):
    b, s, _ = proj.shape
    q, k, v = jnp.split(proj, [C_HEADS * C_HD, (C_HEADS + C_KV_HEADS) * C_HD], axis=-1)
    q = q.reshape(b, s, C_HEADS, C_HD)
    k = k.reshape(b, s, C_KV_HEADS, C_HD)
    v = v.reshape(b, s, C_KV_HEADS, C_HD)
    q = axial_rope(rms_norm(q, qn_g), row, col)
    k = axial_rope(rms_norm(k, kn_g), row, col)
    q = q.reshape(b, s, C_KV_HEADS, C_HEADS // C_KV_HEADS, C_HD)
    return block_attention(q, k, v)


def mla_mixer(proj, q_a_g, w_q_b, kv_a_g, w_kv_b, qn_g, kn_g, row, col):
    b, s, _ = proj.shape
    q_a, c_kv, k_rope = jnp.split(proj, [Q_LORA, Q_LORA + KV_LORA], axis=-1)
    q = (rms_norm(q_a, q_a_g) @ w_q_b).reshape(b, s, D_HEADS, NOPE + ROPE_D)
    kv = (rms_norm(c_kv, kv_a_g) @ w_kv_b).reshape(b, s, D_HEADS, NOPE + V_HD)
    k_nope, v = jnp.split(kv, [NOPE], axis=-1)
    k = jnp.concatenate([k_nope, jnp.broadcast_to(k_rope[:, :, None, :], (b, s, D_HEADS, ROPE_D))], axis=-1)
    q = rms_norm(q, qn_g)
    k = rms_norm(k, kn_g)
    q = jnp.concatenate([q[..., :NOPE], axial_rope(q[..., NOPE:], row, col)], axis=-1)
    k = jnp.concatenate([k[..., :NOPE], axial_rope(k[..., NOPE:], row, col)], axis=-1)
    return block_attention(q[:, :, :, None, :], k, v)


def run_trunk(x, c, p):
    b, s, _ = x.shape
    rows = s // GRID_W
    row = jnp.repeat(jnp.arange(rows, dtype=jnp.float32), GRID_W)
    col = jnp.tile(jnp.arange(GRID_W, dtype=jnp.float32), rows)
    for l in range(DEPTH):
        mod = jax.nn.silu(c) @ p['ada_w'][l] + p['ada_b'][l]
        sh1, sc1, g1, sh2, sc2, g2 = [m[:, None, :] for m in jnp.split(mod, 6, axis=-1)]
        h = rms_norm(x, p['norm1_g'][l]) * (1 + sc1) + sh1
        proj = h @ p['w_in'][l]
        pa, pb, pc, pd = jnp.split(proj, [IN_A, IN_A + IN_B, IN_A + IN_B + IN_C], axis=-1)
        ya = gmlp_mixer(pa, p['gm_vnorm_g'][l], p['gm_spatial_w'][l], p['gm_spatial_b'][l])
        yb = hyena_mixer(pb, p['hy_conv_w'][l], p['hy_conv_b'][l], p['hy_w1'][l], p['hy_b1'][l], p['hy_f1'][l],
                         p['hy_w2'][l], p['hy_b2'][l], p['hy_f2'][l], p['hy_w3'][l], p['hy_decay'][l], p['hy_skip'][l])
        yc = gqa_mixer(pc, p['gqa_qn_g'][l], p['gqa_kn_g'][l], row, col)
        yd = mla_mixer(pd, p['mla_q_a_g'][l], p['mla_w_q_b'][l], p['mla_kv_a_g'][l], p['mla_w_kv_b'][l],
                       p['mla_qn_g'][l], p['mla_kn_g'][l], row, col)
        ycat = jnp.stack([ya, yb, yc, yd], axis=2)
        ycat = rms_norm(ycat, p['group_norm_g'][l].reshape(N_GROUPS, W_GROUP)).reshape(b, s, D_MIX)
        x = x + g1 * (ycat @ p['w_out'][l])
        h = rms_norm(x, p['norm2_g'][l]) * (1 + sc2) + sh2
        u = dwconv3(h @ p['ffn_w_up'][l], p['ffn_conv_w'][l], p['ffn_conv_b'][l])
        gate, up = jnp.split(u, 2, axis=-1)
        x = x + g2 * ((jax.nn.silu(gate) * up) @ p['ffn_w_down'][l])
    return x


def setup_inputs(seed: int = 0) -> dict:
    key = jax.random.key(seed)
    ks = iter(jax.random.split(key, 48))

    def nrm(shape, scale):
        return jax.random.normal(next(ks), shape, jnp.float32) * scale

    def gain(shape):
        return 1.0 + nrm(shape, 0.02)

    L = DEPTH
    decay_base = jnp.abs(jnp.linspace(math.log(HYENA_TARGET) / HYENA_DECAY_LONG,
                                      math.log(HYENA_TARGET) / HYENA_DECAY_SHORT, B_W, dtype=jnp.float32))
    decay_base = jnp.tile(decay_base, HYENA_ORDER * 2)
    return {
        'x_prompt': nrm((BATCH, SEQ, D_MODEL), 1.0),
        'x_sample': nrm((DEC_BATCH, DEC_SEQ, D_MODEL), 1.0),
        'c_prompt': nrm((BATCH, D_MODEL), 1.0),
        'c_sample': nrm((DEC_BATCH, D_MODEL), 1.0),
        'ada_w': nrm((L, D_MODEL, 6 * D_MODEL), 0.5 * D_MODEL ** -0.5),
        'ada_b': nrm((L, 6 * D_MODEL), 0.02),
        'norm1_g': gain((L, D_MODEL)),
        'w_in': nrm((L, D_MODEL, IN_COLS), D_MODEL ** -0.5),
        'gm_vnorm_g': gain((L, W_GROUP)),
        'gm_spatial_w': nrm((L, A_HEADS, CHUNK, CHUNK), CHUNK ** -0.5),
        'gm_spatial_b': gain((L, A_HEADS, CHUNK)),
        'hy_conv_w': nrm((L, 3, IN_B), 3 ** -0.5),
        'hy_conv_b': nrm((L, IN_B), 0.02),
        'hy_w1': nrm((L, POS_EMB, FILT_H), POS_EMB ** -0.5),
        'hy_b1': nrm((L, FILT_H), 0.1),
        'hy_f1': gain((L, FILT_H)),
        'hy_w2': nrm((L, FILT_H, FILT_H), FILT_H ** -0.5),
        'hy_b2': nrm((L, FILT_H), 0.1),
        'hy_f2': gain((L, FILT_H)),
        'hy_w3': nrm((L, FILT_H, HYENA_ORDER * 2 * B_W), 0.1 * FILT_H ** -0.5),
        'hy_decay': decay_base * (1.0 + nrm((L, HYENA_ORDER * 2 * B_W), 0.05)),
        'hy_skip': nrm((L, HYENA_ORDER, B_W), 0.5),
        'gqa_qn_g': gain((L, C_HD)),
        'gqa_kn_g': gain((L, C_HD)),
        'mla_q_a_g': gain((L, Q_LORA)),
        'mla_w_q_b': nrm((L, Q_LORA, D_HEADS * (NOPE + ROPE_D)), Q_LORA ** -0.5),
        'mla_kv_a_g': gain((L, KV_LORA)),
        'mla_w_kv_b': nrm((L, KV_LORA, D_HEADS * (NOPE + V_HD)), KV_LORA ** -0.5),
        'mla_qn_g': gain((L, NOPE + ROPE_D)),
        'mla_kn_g': gain((L, NOPE + ROPE_D)),
        'group_norm_g': gain((L, D_MIX)),
        'w_out': nrm((L, D_MIX, D_MODEL), D_MIX ** -0.5),
        'norm2_g': gain((L, D_MODEL)),
        'ffn_w_up': nrm((L, D_MODEL, 2 * D_FF), D_MODEL ** -0.5),
        'ffn_conv_w': nrm((L, 3, 2 * D_FF), 3 ** -0.5),
        'ffn_conv_b': nrm((L, 2 * D_FF), 0.02),
        'ffn_w_down': nrm((L, D_FF, D_MODEL), D_FF ** -0.5),
    }


def reference(x_prompt, x_sample, c_prompt, c_sample, ada_w, ada_b, norm1_g, w_in, gm_vnorm_g, gm_spatial_w,
              gm_spatial_b, hy_conv_w, hy_conv_b, hy_w1, hy_b1, hy_f1, hy_w2, hy_b2, hy_f2, hy_w3, hy_decay, hy_skip,
              gqa_qn_g, gqa_kn_g, mla_q_a_g, mla_w_q_b, mla_kv_a_g, mla_w_kv_b, mla_qn_g, mla_kn_g, group_norm_g,
              w_out, norm2_g, ffn_w_up, ffn_conv_w, ffn_conv_b, ffn_w_down):
    params = {
        'ada_w': ada_w, 'ada_b': ada_b, 'norm1_g': norm1_g, 'w_in': w_in,
        'gm_vnorm_g': gm_vnorm_g, 'gm_spatial_w': gm_spatial_w, 'gm_spatial_b': gm_spatial_b,
        'hy_conv_w': hy_conv_w, 'hy_conv_b': hy_conv_b, 'hy_w1': hy_w1, 'hy_b1': hy_b1, 'hy_f1': hy_f1,
        'hy_w2': hy_w2, 'hy_b2': hy_b2, 'hy_f2': hy_f2, 'hy_w3': hy_w3, 'hy_decay': hy_decay, 'hy_skip': hy_skip,
        'gqa_qn_g': gqa_qn_g, 'gqa_kn_g': gqa_kn_g,
        'mla_q_a_g': mla_q_a_g, 'mla_w_q_b': mla_w_q_b, 'mla_kv_a_g': mla_kv_a_g, 'mla_w_kv_b': mla_w_kv_b,
        'mla_qn_g': mla_qn_g, 'mla_kn_g': mla_kn_g,
        'group_norm_g': group_norm_g, 'w_out': w_out, 'norm2_g': norm2_g,
        'ffn_w_up': ffn_w_up, 'ffn_conv_w': ffn_conv_w, 'ffn_conv_b': ffn_conv_b, 'ffn_w_down': ffn_w_down,
    }
    y_prompt = run_trunk(x_prompt, c_prompt, params)
    y_sample = run_trunk(x_sample, c_sample, params)
    return (y_prompt, y_sample)
```

```python
import math
from contextlib import ExitStack
import numpy as np
import ml_dtypes
import concourse.bass as bass
import concourse.mybir as mybir
from concourse.bass_utils import run_bass_kernel_spmd

F32 = mybir.dt.float32
BF16 = mybir.dt.bfloat16
AF = mybir.ActivationFunctionType
ALU = mybir.AluOpType
AX = mybir.AxisListType

D = 2048
KC = 16
IN_COLS = 4416
DFF = 5632
EPS = 1e-6
SAME_ENG_SYNC = True
WEIGHT_NAMES = ['ada_w', 'ada_b', 'norm1_g', 'w_in', 'gm_vnorm_g', 'gm_spatial_w', 'gm_spatial_b', 'hy_conv_w',
                'hy_conv_b', 'hy_w1', 'hy_b1', 'hy_f1', 'hy_w2', 'hy_b2', 'hy_f2', 'hy_w3', 'hy_decay', 'hy_skip',
                'gqa_qn_g', 'gqa_kn_g', 'mla_q_a_g', 'mla_w_q_b', 'mla_kv_a_g', 'mla_w_kv_b', 'mla_qn_g',
                'mla_kn_g', 'group_norm_g', 'w_out', 'norm2_g', 'ffn_w_up', 'ffn_conv_w', 'ffn_conv_b', 'ffn_w_down']


class B:
    __slots__ = ('w', 'r', 'dsem', 'dcnt', 'bg')

    def __init__(self, bg=False):
        self.w = None
        self.r = {}
        self.dsem = None
        self.dcnt = 0
        self.bg = bg


class Prog:
    def __init__(self, nc, es):
        self.nc = nc
        self.es = es
        self.eng = {'pe': nc.tensor, 'act': nc.scalar, 'dve': nc.vector, 'pool': nc.gpsimd, 'sp': nc.sync}
        self.sem = {k: es.enter_context(nc.semaphore('sem_' + k)) for k in ('pe', 'act', 'dve', 'pool')}
        self.cnt = {k: 0 for k in self.sem}
        self.waited = {k: {} for k in self.eng}
        self.dsems = []
        self.free_ds = []
        self.old_sems = []
        self.all_ds = []
        self.nsem = 0

    def _wait(self, eng, sem, val):
        w = self.waited[eng]
        if w.get(id(sem), 0) < val:
            self.eng[eng].wait_ge(sem, val)
            w[id(sem)] = val

    def op(self, eng, fn, r=(), w=(), dma=None):
        deps = {}
        for b in r:
            if b.w is not None:
                s, v = b.w
                if deps.get(id(s), (None, 0))[1] < v:
                    deps[id(s)] = (s, v)
        for b in w:
            if b.w is not None:
                s, v = b.w
                if deps.get(id(s), (None, 0))[1] < v:
                    deps[id(s)] = (s, v)
            for s, v in b.r.values():
                if deps.get(id(s), (None, 0))[1] < v:
                    deps[id(s)] = (s, v)
        for s, v in deps.values():
            if eng == 'pe' and s is self.sem['pe']:
                continue
            if not SAME_ENG_SYNC and eng in self.sem and s is self.sem[eng]:
                continue
            self._wait(eng, s, v)
        ins = fn(self.eng[eng])
        if dma is None:
            self.cnt[eng] += 1
            tok = (self.sem[eng], self.cnt[eng])
            ins.then_inc(self.sem[eng], 1)
        else:
            if dma.dsem is None and dma.bg:
                dma.dsem = [self.es.enter_context(self.nc.semaphore('bg%d' % self.nsem)), 0]
                self.nsem += 1
            if dma.dsem is None:
                if self.free_ds:
                    dma.dsem = self.free_ds.pop()
                else:
                    dma.dsem = [self.es.enter_context(self.nc.semaphore('ds%d' % self.nsem)), 0]
                    self.nsem += 1
                    self.all_ds.append(dma.dsem)
                self.dsems.append(dma)
            dma.dsem[1] += 16
            tok = (dma.dsem[0], dma.dsem[1])
            ins.then_inc(dma.dsem[0], 16)
        for b in r:
            s, v = tok
            if b.r.get(id(s), (None, 0))[1] < v:
                b.r[id(s)] = tok
        for b in w:
            b.w = tok
            b.r = {}
        return tok

    def rotate(self):
        for k in list(self.sem):
            self.old_sems.append((self.sem[k], self.cnt[k]))
            self.sem[k] = self.es.enter_context(self.nc.semaphore('sem_%s_%d' % (k, len(self.old_sems))))
            self.cnt[k] = 0

    def barrier(self):
        for e in self.eng:
            for k in self.sem:
                if self.cnt[k]:
                    self._wait(e, self.sem[k], self.cnt[k])
            for d in self.all_ds:
                if d[1]:
                    self._wait(e, d[0], d[1])
        for b in self.dsems:
            self.free_ds.append(b.dsem)
            b.dsem = None
        self.dsems = []


_TN = [0]


def run_pipelined(gens, depth):
    it = iter(gens)
    active = []
    done = False
    while True:
        if not done and len(active) < depth:
            g = next(it, None)
            if g is None:
                done = True
            else:
                active.append(g)
        if not active:
            if done:
                break
            continue
        for g in list(active):
            try:
                next(g)
            except StopIteration:
                active.remove(g)


class T:
    def __init__(self, es, nc, name, shape, dt):
        _TN[0] += 1
        self.t = es.enter_context(nc.sbuf_tensor('%s_%d' % (name, _TN[0]), shape, dt))
        self.b = B()


def build(S, DEPTH):
    NT = S // 128
    TB = min(512, S)
    NB = S // TB
    TPB = TB // 128
    NF = NT
    FC = S // 2
    Lp = S // 2
    nc = bass.Bass("TRN2", target_bir_lowering=False)
    es = ExitStack()
    with es:
        P = Prog(nc, es)
        op = P.op

        def din(name, shape, dt=F32):
            return nc.dram_tensor(name, shape, dt, kind="ExternalInput").ap()

        def dscr(name, shape, dt=F32):
            return nc.dram_tensor(name, shape, dt, kind="Internal").ap()

        x_in = din('x', [S, D])
        c_in = din('c', [D])
        y_out = nc.dram_tensor('y', [S, D], F32, kind="ExternalOutput").ap()
        Wd = {}
        shapes = {'ada_w': [D, 6 * D], 'ada_b': [6 * D], 'norm1_g': [D], 'w_in': [D, IN_COLS], 'gm_vnorm_g': [512],
                  'gm_spatial_w': [4, 128, 128], 'gm_spatial_b': [4, 128], 'hy_conv_w': [3, 1536], 'hy_conv_b': [1536],
                  'hy_w1': [33, 64], 'hy_b1': [64], 'hy_f1': [64], 'hy_w2': [64, 64], 'hy_b2': [64], 'hy_f2': [64],
                  'hy_w3': [64, 2048], 'hy_decay': [2048], 'hy_skip': [2, 512], 'gqa_qn_g': [128], 'gqa_kn_g': [128],
                  'mla_q_a_g': [512], 'mla_w_q_b': [512, 768], 'mla_kv_a_g': [256], 'mla_w_kv_b': [256, 1024],
                  'mla_qn_g': [192], 'mla_kn_g': [192], 'group_norm_g': [D], 'w_out': [D, D], 'norm2_g': [D],
                  'ffn_w_up': [D, 2 * DFF], 'ffn_conv_w': [3, 2 * DFF], 'ffn_conv_b': [2 * DFF], 'ffn_w_down': [DFF, D]}
        for n in WEIGHT_NAMES:
            Wd[n] = din(n, [DEPTH] + shapes[n])
        c_ident = din('k_ident', [128, 128])
        c_ones = din('k_ones', [128, 128], BF16)
        c_permG = din('k_permG', [128, 128], BF16)
        c_permM = din('k_permM', [64, 64], BF16)
        c_cosG = din('k_cosG', [128, S]); c_sinG = din('k_sinG', [128, S])
        c_cosM = din('k_cosM', [64, S]); c_sinM = din('k_sinM', [64, S])
        c_feats = din('k_featsT', [33, S])
        c_ntau = din('k_negtau', [128, NT])
        c_kmask = din('k_keymask', [128, NT])
        c_tmask = din('k_tokmask', [128, NT])
        c_mcol = din('k_mcol', [128, 1])
        c_Fre = din('k_Fre', [NF + 1, 128, NT, 128], BF16)
        c_Fim = din('k_Fim', [NF, 128, NT, 128], BF16)
        c_Ire = din('k_Ire', [NT, 128, NF + 1, 128], BF16)
        c_Iim = din('k_Iim', [NT, 128, NF, 128], BF16)
        mod_scr = dscr('mod_scr', [DEPTH, 6 * D]); mod_b = B()
        proj_scr = dscr('proj_scr', [IN_COLS, S]); proj_b = [B() for _ in range(35)]
        ycat_scr = dscr('ycat_scr', [D, S], BF16); ycat_b = B()
        filt_scr = dscr('filt_scr', [4, S, 512], BF16); filt_b = B()
        hy_scr = dscr('hy_scr', [3, S, 512]); hy_b = B()
        y1_scr = dscr('y1_scr', [S, 512]); y1_b = B()
        z_scr = dscr('z_scr', [2 * NF + 1, 128, 512], BF16); z_b = B()
        GB = min(TB, FC)
        g_scr = dscr('g_scr', [S // GB, 128, 44, GB], BF16); g_b = B()
        res_b = [B() for _ in range(NT)]
        wbf = {}
        wbf_b = {}
        for n_, shp, npc in (('w_in', [D, IN_COLS], 2), ('w_out', [D, D], 1), ('ffn_w_up', [D, 2 * DFF], 4), ('ffn_w_down', [DFF, D], 4)):
            wbf[n_] = [dscr('%s_bf%d' % (n_, l_), shp, BF16) for l_ in range(DEPTH)]
            wbf_b[n_] = [[B(bg=True) for _ in range(npc)] for l_ in range(DEPTH)]

        PS = [es.enter_context(nc.psum_tensor('ps%d' % i, [128, 512], F32)) for i in range(8)]
        PSB = [B() for _ in range(8)]
        ident = T(es, nc, 'ident', [128, 128], F32)
        identb = T(es, nc, 'identb', [128, 128], BF16)
        ones = T(es, nc, 'ones', [128, 128], BF16)
        cst = T(es, nc, 'cst', [128, 8], F32)
        kmask = T(es, nc, 'kmask', [128, NT], F32)
        kbG = T(es, nc, 'kbG', [128, NT], F32)
        kbM = T(es, nc, 'kbM', [128, NT], F32)
        tmask = T(es, nc, 'tmask', [128, NT], F32)
        ntau = T(es, nc, 'ntau', [128, NT], F32)
        mcol = T(es, nc, 'mcol', [128, 1], F32)
        gcol = T(es, nc, 'gcol', [128, 64], F32)

        def ld(dst_t, dst_ap, src_ap, rb=(), eng='sp', slow=False):
            if slow:
                return op(eng, lambda e: e.dma_start(out=dst_ap, in_=src_ap, allow_slow_non_contiguous=True), r=rb, w=(dst_t.b,), dma=dst_t.b)
            return op(eng, lambda e: e.dma_start(out=dst_ap, in_=src_ap), r=rb, w=(dst_t.b,), dma=dst_t.b)

        def st(src_t, src_ap, dst_ap, wb=(), eng='act'):
            return op(eng, lambda e: e.dma_start(out=dst_ap, in_=src_ap), r=(src_t.b,), w=wb, dma=src_t.b)

        for l_ in range(DEPTH):
            for n_ in ('w_in', 'w_out', 'ffn_w_up', 'ffn_w_down'):
                bl = wbf_b[n_][l_]
                rows_ = wbf[n_][l_].shape[0]
                step = rows_ // len(bl)
                for pi_, b_ in enumerate(bl):
                    op('pool', lambda e, n_=n_, l_=l_, pi_=pi_, step=step: e.dma_start(out=wbf[n_][l_][pi_ * step:(pi_ + 1) * step, :],
                                                                                       in_=Wd[n_][l_][pi_ * step:(pi_ + 1) * step, :]),
                       r=(), w=(b_,), dma=b_)
        ld(ident, ident.t[:], c_ident)
        ld(ones, ones.t[:], c_ones)
        op('dve', lambda e: e.tensor_copy(out=identb.t[:], in_=ident.t[:]), r=(ident.b,), w=(identb.b,))
        ld(kmask, kmask.t[:], c_kmask)
        ld(tmask, tmask.t[:], c_tmask)
        ld(ntau, ntau.t[:], c_ntau)
        ld(mcol, mcol.t[:], c_mcol)
        op('dve', lambda e: e.memset(cst.t[:, 0:1], EPS), w=(cst.b,))
        op('dve', lambda e: e.memset(cst.t[:, 3:4], 0.0), w=(cst.b,))
        SHG = math.sqrt(128.0) * 0.5
        SHM = math.sqrt(192.0) * 0.5
        op('dve', lambda e: e.tensor_scalar(out=kbG.t[:], in0=kmask.t[:], scalar1=-SHG, scalar2=None, op0=ALU.add),
           r=(kmask.b,), w=(kbG.b,))
        op('dve', lambda e: e.tensor_scalar(out=kbM.t[:], in0=kmask.t[:], scalar1=-SHM, scalar2=None, op0=ALU.add),
           r=(kmask.b,), w=(kbM.b,))

        psrr = [0]
        ps_allowed = [list(range(8))]

        def psget():
            al = ps_allowed[0]
            i = al[psrr[0] % len(al)]
            psrr[0] += 1
            return PS[i], PSB[i]

        with ExitStack() as ph:
            ccol = T(ph, nc, 'ccol', [128, KC], F32)
            brow = T(ph, nc, 'brow', [1, 6 * D], F32)
            mrow = T(ph, nc, 'mrow', [1, 6 * D], F32)
            wb = [T(ph, nc, 'adaw%d' % i, [128, KC, 512], F32) for i in range(2)]
            ld(ccol, ccol.t[:], c_in.rearrange("(k p) -> p k", p=128), slow=True)
            op('act', lambda e: e.activation(out=ccol.t[:], in_=ccol.t[:], func=AF.Silu), r=(), w=(ccol.b,))
            it = 0
            for l in range(DEPTH):
                ld(brow, brow.t[:], Wd['ada_b'][l].rearrange("(o n) -> o n", o=1))
                for j in range(24):
                    wt = wb[it % 2]; it += 1
                    ld(wt, wt.t[:], Wd['ada_w'][l][:, j * 512:(j + 1) * 512].rearrange("(k p) n -> p k n", p=128))
                    ps, psb = psget()

                    def f(e, ps=ps, wt=wt):
                        for k in range(KC):
                            ins = e.matmul(ps[0:1, :], ccol.t[:, k:k + 1], wt.t[:, k, :], start=(k == 0), stop=(k == KC - 1))
                        return ins
                    op('pe', f, r=(ccol.b, wt.b), w=(psb,))
                    op('dve', lambda e, ps=ps, j=j: e.tensor_tensor(out=mrow.t[:, j * 512:(j + 1) * 512], in0=ps[0:1, :],
                                                                     in1=brow.t[:, j * 512:(j + 1) * 512], op=ALU.add),
                       r=(psb, brow.b), w=(mrow.b,))
                st(mrow, mrow.t[:], mod_scr[l].rearrange("(o n) -> o n", o=1), wb=(mod_b,))
            P.barrier()

        def colvec(dst_col, src_ap, n, eng='sp', extra_r=()):
            k = (n + 127) // 128
            p = min(n, 128)
            return ld(gcol, gcol.t[0:p, dst_col:dst_col + k], src_ap.rearrange("(k p) -> p k", p=p), rb=extra_r, slow=True)

        def norm_phase(ph, l, which, hT, col_of_tile, src_is_x):
            base = 0 if which == 0 else 3
            Abc = T(ph, nc, 'nAbc', [128, D], F32)
            Bbc = T(ph, nc, 'nBbc', [128, D], F32)
            ld(Abc, Abc.t[:], mod_scr[l][(base + 1) * D:(base + 2) * D].partition_broadcast(128), rb=(mod_b,))
            ld(Bbc, Bbc.t[:], Wd['norm1_g' if which == 0 else 'norm2_g'][l].partition_broadcast(128))
            op('dve', lambda e: e.scalar_tensor_tensor(out=Abc.t[:], in0=Abc.t[:], scalar=1.0, in1=Bbc.t[:], op0=ALU.add, op1=ALU.mult),
               r=(Bbc.b,), w=(Abc.b,))
            ld(Bbc, Bbc.t[:], mod_scr[l][base * D:(base + 1) * D].partition_broadcast(128), rb=(mod_b,))
            xt = [T(ph, nc, 'nx%d' % i, [128, D], F32) for i in range(3)]
            xb = [T(ph, nc, 'nxb%d' % i, [128, D], BF16) for i in range(3)]
            junk = T(ph, nc, 'njunk', [128, D], BF16)
            ss = [T(ph, nc, 'nss%d' % i, [128, 2], F32) for i in range(3)]
            def tile_gen(i, tt, dc, sc, ncl):
                x = xt[i % 3]; s_ = ss[i % 3]; xb_ = xb[i % 3]
                src = x_in if src_is_x else y_out
                ld(x, x.t[:], src[tt * 128:(tt + 1) * 128, :], rb=(res_b[tt],))
                op('dve', lambda e, s_=s_: e.memset(s_.t[:], 0.0), w=(s_.b,))
                op('act', lambda e, x=x, s_=s_: e.activation(out=junk.t[:], in_=x.t[:], func=AF.Square, accum_out=s_.t[:, 0:1]),
                   r=(x.b,), w=(junk.b, s_.b))
                op('act', lambda e, s_=s_: e.activation(out=s_.t[:, 1:2], in_=s_.t[:, 0:1], func=AF.Sqrt, scale=1.0 / D,
                                                        bias=cst.t[:, 0:1]), r=(cst.b,), w=(s_.b,))
                yield
                op('dve', lambda e, s_=s_: e.reciprocal(out=s_.t[:, 1:2], in_=s_.t[:, 1:2]), w=(s_.b,))
                op('dve', lambda e, x=x, s_=s_: e.scalar_tensor_tensor(out=x.t[:], in0=x.t[:], scalar=s_.t[:, 1:2], in1=Abc.t[:],
                                                                         op0=ALU.mult, op1=ALU.mult), r=(s_.b, Abc.b), w=(x.b,))
                op('dve', lambda e, x=x, xb_=xb_: e.tensor_tensor(out=xb_.t[:], in0=x.t[:], in1=Bbc.t[:], op=ALU.add), r=(x.b, Bbc.b), w=(xb_.b,))
                yield
                for q in range(2):
                    ps, psb = psget()
                    psv = ps[:, :].bitcast(BF16)

                    def f(e, psv=psv, xb_=xb_, q=q):
                        for j in range(8):
                            k = q * 8 + j
                            ins = e.transpose(psv[:, j * 128:(j + 1) * 128], xb_.t[:, k * 128:(k + 1) * 128], identb.t[:])
                        return ins
                    op('pe', f, r=(xb_.b, identb.b), w=(psb,))
                    src_ap = psv.rearrange("p (j t) -> p j t", t=128)[:, :, sc:sc + ncl]
                    dst_ap = hT.t[:, q * 8:(q + 1) * 8, dc:dc + ncl]
                    if q == 0:
                        op('act', lambda e, src_ap=src_ap, dst_ap=dst_ap: e.copy(out=dst_ap, in_=src_ap), r=(psb,), w=(hT.b,))
                    else:
                        op('dve', lambda e, src_ap=src_ap, dst_ap=dst_ap: e.tensor_copy(out=dst_ap, in_=src_ap), r=(psb,), w=(hT.b,))
            run_pipelined((tile_gen(i, *c) for i, c in enumerate(col_of_tile)), 3)

        def fm_sumsq(parts, n, ps, psb, sq_tiles):
            for i, (t_, ap) in enumerate(parts):
                sq = sq_tiles[i % len(sq_tiles)]
                p = ap.shape[0]
                op('act', lambda e, sq=sq, ap=ap, p=p: e.activation(out=sq.t[0:p, 0:n], in_=ap, func=AF.Square),
                   r=(t_.b,), w=(sq.b,))
            def f(e):
                for i, (t_, ap) in enumerate(parts):
                    p = ap.shape[0]
                    ins = e.matmul(ps[:, 0:n], ones.t[0:p, :], sq_tiles[i].t[0:p, 0:n], start=(i == 0), stop=(i == len(parts) - 1))
                return ins
            op('pe', f, r=tuple(sq_tiles[i].b for i in range(len(parts))) + (ones.b,), w=(psb,))

        def rstd_from(ps, psb, n, dim, out_t):
            op('act', lambda e: e.activation(out=out_t.t[:, 0:n], in_=ps[:, 0:n], func=AF.Sqrt, scale=1.0 / dim, bias=cst.t[:, 0:1]),
               r=(psb, cst.b), w=(out_t.b,))
            op('dve', lambda e: e.reciprocal(out=out_t.t[:, 0:n], in_=out_t.t[:, 0:n]), w=(out_t.b,))

        for l in range(DEPTH):
            first = (l == 0)

            with ExitStack() as ph:
                hT = T(ph, nc, 'hT', [128, KC, S], BF16)
                with ExitStack() as ph2:
                    norm_phase(ph2, l, 0, hT, [(tt, tt * 128, 0, 128) for tt in range(NT)], first)
                    P.barrier()
                wbuf = [T(ph, nc, 'winb%d' % i, [128, KC, 256], BF16) for i in range(2)]
                rows = [T(ph, nc, 'prow%d' % i, [128, S], F32) for i in range(2)]
                for cc in range(35):
                    c0 = cc * 128
                    cw = min(128, IN_COLS - c0)
                    wt = wbuf[(cc // 2) % 2]; row = rows[cc % 2]
                    wo_ = (cc % 2) * 128
                    if cc % 2 == 0:
                        bw = min(256, IN_COLS - c0)
                        ld(wt, wt.t[:, :, 0:bw], wbf['w_in'][l][:, c0:c0 + bw].rearrange("(k p) n -> p k n", p=128), rb=tuple(wbf_b['w_in'][l]))
                    for tb in range(NB):
                        ps, psb = psget()

                        def f(e, ps=ps, wt=wt, tb=tb, cw=cw, wo_=wo_):
                            for k in range(KC):
                                ins = e.matmul(ps[0:cw, 0:TB], wt.t[:, k, wo_:wo_ + cw], hT.t[:, k, tb * TB:(tb + 1) * TB],
                                               start=(k == 0), stop=(k == KC - 1))
                            return ins
                        op('pe', f, r=(wt.b, hT.b), w=(psb,))
                        en = 'act' if tb % 2 == 0 else 'dve'
                        if en == 'act':
                            op('act', lambda e, ps=ps, row=row, tb=tb, cw=cw: e.copy(out=row.t[0:cw, tb * TB:(tb + 1) * TB], in_=ps[0:cw, 0:TB]),
                               r=(psb,), w=(row.b,))
                        else:
                            op('dve', lambda e, ps=ps, row=row, tb=tb, cw=cw: e.tensor_copy(out=row.t[0:cw, tb * TB:(tb + 1) * TB], in_=ps[0:cw, 0:TB]),
                               r=(psb,), w=(row.b,))
                    st(row, row.t[0:cw, :], proj_scr[c0:c0 + cw, :], wb=(proj_b[cc],))
                P.barrier()

            with ExitStack() as ph:
                wsT = T(ph, nc, 'wsT', [128, 4, 128], BF16)
                wsl = T(ph, nc, 'wsl', [128, 4, 128], F32)
                bsT = T(ph, nc, 'bsT', [128, 4], F32)
                vgbc = T(ph, nc, 'vgbc', [128, 512], F32)
                gnbc = T(ph, nc, 'gnbcA', [128, 512], F32)
                ld(gnbc, gnbc.t[:], Wd['group_norm_g'][l][0:512].partition_broadcast(128))
                ld(wsl, wsl.t[:], Wd['gm_spatial_w'][l].rearrange("h p q -> p h q"))
                ld(bsT, bsT.t[:], Wd['gm_spatial_b'][l].rearrange("h p -> p h"), slow=True)
                ld(vgbc, vgbc.t[:], Wd['gm_vnorm_g'][l].partition_broadcast(128))
                ps, psb = psget()

                def f(e, ps=ps):
                    for h in range(4):
                        ins = e.transpose(ps[:, h * 128:(h + 1) * 128], wsl.t[:, h, :], ident.t[:])
                    return ins
                op('pe', f, r=(wsl.b, ident.b), w=(psb,))
                op('dve', lambda e, ps=ps: e.tensor_copy(out=wsT.t[:].rearrange("p h q -> p (h q)"), in_=ps[:, :]), r=(psb,), w=(wsT.b,))
                uv = [T(ph, nc, 'uv%d' % i, [128, 8, 128], F32) for i in range(2)]
                u_sb2 = [T(ph, nc, 'u_sb%d' % i, [128, 512], F32) for i in range(2)]
                v_sb2 = [T(ph, nc, 'v_sb%d' % i, [128, 512], F32) for i in range(2)]
                vn2 = [T(ph, nc, 'vn%d' % i, [128, 512], BF16) for i in range(2)]
                ya2 = [T(ph, nc, 'ya%d' % i, [128, 512], F32) for i in range(2)]
                yst = [T(ph, nc, 'yst%d' % i, [128, 4, 128], BF16) for i in range(2)]
                sm2 = [T(ph, nc, 'sm%d' % i, [128, 8], F32) for i in range(2)]
                junk2 = [T(ph, nc, 'ajunk%d' % i, [128, 512], BF16) for i in range(2)]
                def a_gen(tt):
                    u_sb = u_sb2[tt % 2]; v_sb = v_sb2[tt % 2]; vn = vn2[tt % 2]; ya = ya2[tt % 2]; sm = sm2[tt % 2]; junk = junk2[tt % 2]
                    t_ = uv[tt % 2]
                    ld(t_, t_.t[:], proj_scr[0:1024, tt * 128:(tt + 1) * 128].rearrange("(c p) t -> p c t", p=128), rb=tuple(proj_b[0:8]))
                    op('act', lambda e, t_=t_: e.activation(out=t_.t[:], in_=t_.t[:], func=AF.Gelu), w=(t_.b,))
                    yield
                    pu, pub = psget()
                    pv, pvb = psget()

                    def f(e, t_=t_, pu=pu, pv=pv):
                        for j in range(4):
                            e.transpose(pu[:, j * 128:(j + 1) * 128], t_.t[:, j, :], ident.t[:])
                        for j in range(4):
                            ins = e.transpose(pv[:, j * 128:(j + 1) * 128], t_.t[:, 4 + j, :], ident.t[:])
                        return ins
                    op('pe', f, r=(t_.b, ident.b), w=(pub, pvb))
                    op('dve', lambda e, pu=pu: e.tensor_copy(out=u_sb.t[:], in_=pu[:, :]), r=(pub,), w=(u_sb.b,))
                    op('dve', lambda e: e.memset(sm.t[:], 0.0), w=(sm.b,))
                    op('act', lambda e, pv=pv: e.activation(out=v_sb.t[:], in_=pv[:, :], func=AF.Identity, accum_out=sm.t[:, 0:1]),
                       r=(pvb,), w=(v_sb.b, sm.b))
                    yield
                    op('dve', lambda e: e.tensor_scalar(out=sm.t[:, 1:2], in0=sm.t[:, 0:1], scalar1=-1.0 / 512, scalar2=None, op0=ALU.mult), w=(sm.b,))
                    op('dve', lambda e: e.tensor_scalar(out=v_sb.t[:], in0=v_sb.t[:], scalar1=sm.t[:, 1:2], scalar2=None, op0=ALU.add),
                       r=(sm.b,), w=(v_sb.b,))
                    op('act', lambda e: e.activation(out=junk.t[:], in_=v_sb.t[:], func=AF.Square, accum_out=sm.t[:, 2:3]),
                       r=(v_sb.b,), w=(junk.b, sm.b))
                    op('act', lambda e: e.activation(out=sm.t[:, 3:4], in_=sm.t[:, 2:3], func=AF.Sqrt, scale=1.0 / 512, bias=cst.t[:, 0:1]),
                       r=(cst.b,), w=(sm.b,))
                    yield
                    op('dve', lambda e: e.reciprocal(out=sm.t[:, 3:4], in_=sm.t[:, 3:4]), w=(sm.b,))
                    op('dve', lambda e: e.scalar_tensor_tensor(out=vn.t[:], in0=v_sb.t[:], scalar=sm.t[:, 3:4], in1=vgbc.t[:],
                                                               op0=ALU.mult, op1=ALU.mult), r=(v_sb.b, sm.b, vgbc.b), w=(vn.b,))
                    yield
                    pm, pmb = psget()

                    def f(e, pm=pm):
                        for h in range(4):
                            ins = e.matmul(pm[:, h * 128:(h + 1) * 128], wsT.t[:, h, :], vn.t[:, h * 128:(h + 1) * 128], start=True, stop=True)
                        return ins
                    op('pe', f, r=(wsT.b, vn.b), w=(pmb,))
                    for h in range(4):
                        op('dve', lambda e, pm=pm, h=h: e.scalar_tensor_tensor(
                            out=ya.t[:, h * 128:(h + 1) * 128], in0=pm[:, h * 128:(h + 1) * 128], scalar=bsT.t[:, h:h + 1],
                            in1=u_sb.t[:, h * 128:(h + 1) * 128], op0=ALU.add, op1=ALU.mult), r=(pmb, bsT.b, u_sb.b), w=(ya.b,))
                    yield
                    op('act', lambda e: e.activation(out=junk.t[:], in_=ya.t[:], func=AF.Square, accum_out=sm.t[:, 4:5]),
                       r=(ya.b,), w=(junk.b, sm.b))
                    op('act', lambda e: e.activation(out=sm.t[:, 5:6], in_=sm.t[:, 4:5], func=AF.Sqrt, scale=1.0 / 512, bias=cst.t[:, 0:1]),
                       r=(cst.b,), w=(sm.b,))
                    yield
                    op('dve', lambda e: e.reciprocal(out=sm.t[:, 5:6], in_=sm.t[:, 5:6]), w=(sm.b,))
                    op('dve', lambda e: e.scalar_tensor_tensor(out=ya.t[:], in0=ya.t[:], scalar=sm.t[:, 5:6], in1=gnbc.t[:, 0:512],
                                                               op0=ALU.mult, op1=ALU.mult), r=(sm.b, gnbc.b), w=(ya.b,))
                    yield
                    pt, ptb = psget()

                    def f(e, pt=pt):
                        for j in range(4):
                            ins = e.transpose(pt[:, j * 128:(j + 1) * 128], ya.t[:, j * 128:(j + 1) * 128], ident.t[:])
                        return ins
                    op('pe', f, r=(ya.b, ident.b), w=(ptb,))
                    ys = yst[tt % 2]
                    op('act', lambda e, pt=pt, ys=ys: e.copy(out=ys.t[:].rearrange("p c t -> p (c t)"), in_=pt[:, :]), r=(ptb,), w=(ys.b,))
                    st(ys, ys.t[:], ycat_scr[0:512, tt * 128:(tt + 1) * 128].rearrange("(c p) t -> p c t", p=128), wb=(ycat_b,))
                run_pipelined((a_gen(tt) for tt in range(NT)), 2)
                P.barrier()

            with ExitStack() as ph:
                ft = T(ph, nc, 'feats', [33, S], F32)
                w1 = T(ph, nc, 'hw1', [33, 64], F32)
                w2 = T(ph, nc, 'hw2', [64, 64], F32)
                w3 = T(ph, nc, 'hw3', [64, 2048], F32)
                cv = T(ph, nc, 'hcv', [64, 8], F32)
                h1 = T(ph, nc, 'h1T', [64, S], F32)
                h2 = T(ph, nc, 'h2T', [64, S], F32)
                sq = T(ph, nc, 'hsq', [64, TB], F32)
                adec = T(ph, nc, 'adec', [128, 2048], F32)
                warg = T(ph, nc, 'warg', [128, 2048], F32)
                hful = T(ph, nc, 'hful', [128, 2048], F32)
                fst = [T(ph, nc, 'fst%d' % i, [128, 4, 512], BF16) for i in range(2)]
                ld(ft, ft.t[:], c_feats)
                ld(w1, w1.t[:], Wd['hy_w1'][l])
                ld(w2, w2.t[:], Wd['hy_w2'][l])
                ld(w3, w3.t[:], Wd['hy_w3'][l])
                for i, n in enumerate(['hy_b1', 'hy_f1', 'hy_b2', 'hy_f2']):
                    ld(cv, cv.t[:, i:i + 1], Wd[n][l].rearrange("(p o) -> p o", o=1), slow=True)
                ld(adec, adec.t[:], Wd['hy_decay'][l].partition_broadcast(128))
                op('act', lambda e: e.activation(out=adec.t[:], in_=adec.t[:], func=AF.Abs), w=(adec.b,))
                for i in range(2):
                    op('dve', lambda e, i=i: e.tensor_scalar(out=cv.t[:, 4 + 2 * i:5 + 2 * i], in0=cv.t[:, 1 + 2 * i:2 + 2 * i], scalar1=1.0 / 3,
                                                             scalar2=None, op0=ALU.mult), w=(cv.b,))
                    op('dve', lambda e, i=i: e.tensor_tensor(out=cv.t[:, 5 + 2 * i:6 + 2 * i], in0=cv.t[:, 4 + 2 * i:5 + 2 * i],
                                                             in1=cv.t[:, 2 * i:2 * i + 1], op=ALU.mult), w=(cv.b,))
                for li, (wt_, src, dst) in enumerate(((w1, ft, h1), (w2, h1, h2))):
                    for tb in range(NB):
                        ps, psb = psget()
                        op('pe', lambda e, ps=ps, wt_=wt_, src=src, tb=tb: e.matmul(ps[0:64, 0:TB], wt_.t[:, :], src.t[:, tb * TB:(tb + 1) * TB],
                                                                                     start=True, stop=True), r=(wt_.b, src.b), w=(psb,))
                        sl = dst.t[:, tb * TB:(tb + 1) * TB]
                        op('act', lambda e, ps=ps, sl=sl, li=li: e.activation(out=sl, in_=ps[0:64, 0:TB], func=AF.Sin, scale=cv.t[:, 4 + 2 * li:5 + 2 * li],
                                                                              bias=cv.t[:, 5 + 2 * li:6 + 2 * li]), r=(psb, cv.b), w=(dst.b,))
                        op('dve', lambda e, sl=sl: e.tensor_tensor(out=sq.t[:, 0:TB], in0=sl, in1=sl, op=ALU.mult), r=(dst.b,), w=(sq.b,))
                        op('dve', lambda e: e.tensor_scalar(out=sq.t[:, 0:TB], in0=sq.t[:, 0:TB], scalar1=-4.0, scalar2=3.0, op0=ALU.mult, op1=ALU.add), w=(sq.b,))
                        op('dve', lambda e, sl=sl: e.tensor_tensor(out=sl, in0=sl, in1=sq.t[:, 0:TB], op=ALU.mult), r=(sq.b,), w=(dst.b,))
                for tt in range(NT):
                    op('dve', lambda e, tt=tt: e.tensor_scalar(out=warg.t[:], in0=adec.t[:], scalar1=ntau.t[:, tt:tt + 1], scalar2=None, op0=ALU.mult),
                       r=(adec.b, ntau.b), w=(warg.b,))
                    op('act', lambda e: e.activation(out=warg.t[:], in_=warg.t[:], func=AF.Exp), w=(warg.b,))
                    for nb in range(4):
                        ps, psb = psget()
                        op('pe', lambda e, ps=ps, tt=tt, nb=nb: e.matmul(ps[:, :], h2.t[:, tt * 128:(tt + 1) * 128], w3.t[:, nb * 512:(nb + 1) * 512],
                                                                         start=True, stop=True), r=(h2.b, w3.b), w=(psb,))
                        op('dve', lambda e, ps=ps, nb=nb: e.tensor_tensor(out=hful.t[:, nb * 512:(nb + 1) * 512], in0=ps[:, :],
                                                                          in1=warg.t[:, nb * 512:(nb + 1) * 512], op=ALU.mult), r=(psb, warg.b), w=(hful.b,))
                    if tt == 0:
                        for o in range(2):
                            op('dve', lambda e, o=o: e.memset(hful.t[0:1, o * 1024 + 512:o * 1024 + 1024], 0.0), w=(hful.b,))
                    fs = fst[tt % 2]
                    for o in range(2):
                        op('dve', lambda e, o=o, fs=fs: e.tensor_tensor(out=fs.t[:, 2 * o, :], in0=hful.t[:, o * 1024:o * 1024 + 512],
                                                                        in1=hful.t[:, o * 1024 + 512:o * 1024 + 1024], op=ALU.add), r=(hful.b,), w=(fs.b,))
                        op('dve', lambda e, o=o, fs=fs: e.tensor_tensor(out=fs.t[:, 2 * o + 1, :], in0=hful.t[:, o * 1024:o * 1024 + 512],
                                                                        in1=hful.t[:, o * 1024 + 512:o * 1024 + 1024], op=ALU.subtract), r=(hful.b,), w=(fs.b,))
                    st(fs, fs.t[:], filt_scr[:, tt * 128:(tt + 1) * 128, :].rearrange("f p c -> p f c"), wb=(filt_b,))
                P.barrier()
            with ExitStack() as ph:
                cw_ = T(ph, nc, 'hcw', [128, 4, 12], F32)
                for i in range(3):
                    ld(cw_, cw_.t[:, i, :], Wd['hy_conv_w'][l][i].rearrange("(c p) -> p c", p=128), slow=True)
                ld(cw_, cw_.t[:, 3, :], Wd['hy_conv_b'][l].rearrange("(c p) -> p c", p=128), slow=True)
                rin = [T(ph, nc, 'hrin%d' % i, [128, S + 2], F32) for i in range(2)]
                racc2 = [T(ph, nc, 'hracc%d' % i, [128, S], F32) for i in range(2)]
                hst = [T(ph, nc, 'hst%d' % i, [128, 4, 128], F32) for i in range(2)]
                for r_ in rin:
                    op('dve', lambda e, r_=r_: e.memset(r_.t[:, 0:1], 0.0), w=(r_.b,))
                    op('dve', lambda e, r_=r_: e.memset(r_.t[:, S + 1:S + 2], 0.0), w=(r_.b,))
                si = 0
                for ch in range(12):
                    r_ = rin[ch % 2]
                    racc = racc2[ch % 2]
                    ld(r_, r_.t[:, 1:S + 1], proj_scr[1024 + ch * 128:1024 + (ch + 1) * 128, :], rb=(proj_b[8 + ch],))
                    op('dve', lambda e, r_=r_: e.tensor_scalar(out=r_.t[:, Lp + 1:Lp + 2], in0=r_.t[:, Lp + 1:Lp + 2], scalar1=mcol.t[:, 0:1],
                                                               scalar2=None, op0=ALU.mult), r=(mcol.b,), w=(r_.b,))
                    op('dve', lambda e, r_=r_, ch=ch: e.tensor_scalar(out=racc.t[:], in0=r_.t[:, 1:S + 1], scalar1=cw_.t[:, 1, ch:ch + 1],
                                                                      scalar2=cw_.t[:, 3, ch:ch + 1], op0=ALU.mult, op1=ALU.add), r=(r_.b, cw_.b), w=(racc.b,))
                    op('dve', lambda e, r_=r_, ch=ch: e.scalar_tensor_tensor(out=racc.t[:], in0=r_.t[:, 0:S], scalar=cw_.t[:, 0, ch:ch + 1], in1=racc.t[:],
                                                                             op0=ALU.mult, op1=ALU.add), r=(r_.b, cw_.b), w=(racc.b,))
                    op('dve', lambda e, r_=r_, ch=ch: e.scalar_tensor_tensor(out=racc.t[:], in0=r_.t[:, 2:S + 2], scalar=cw_.t[:, 2, ch:ch + 1], in1=racc.t[:],
                                                                             op0=ALU.mult, op1=ALU.add), r=(r_.b, cw_.b), w=(racc.b,))
                    part = ch // 4
                    cch = ch % 4
                    for tq in range(NT // 4):
                        ps, psb = psget()

                        def f(e, ps=ps, tq=tq):
                            for j in range(4):
                                tt = tq * 4 + j
                                ins = e.transpose(ps[:, j * 128:(j + 1) * 128], racc.t[:, tt * 128:(tt + 1) * 128], ident.t[:])
                            return ins
                        op('pe', f, r=(racc.b, ident.b), w=(psb,))
                        hs = hst[si % 2]; si += 1
                        if part == 2:
                            for j in range(4):
                                tt = tq * 4 + j
                                op('dve', lambda e, ps=ps, hs=hs, j=j, tt=tt: e.tensor_scalar(out=hs.t[:, j, :], in0=ps[:, j * 128:(j + 1) * 128],
                                                                                              scalar1=tmask.t[:, tt:tt + 1], scalar2=None, op0=ALU.mult),
                                   r=(psb, tmask.b), w=(hs.b,))
                        else:
                            op('act', lambda e, ps=ps, hs=hs: e.copy(out=hs.t[:].rearrange("p j c -> p (j c)"), in_=ps[:, :]), r=(psb,), w=(hs.b,))
                        st(hs, hs.t[:], hy_scr[part, tq * 512:(tq + 1) * 512, cch * 128:(cch + 1) * 128].rearrange("(j p) c -> p j c", p=128), wb=(hy_b,))
                P.barrier()
            for o in range(2):
                with ExitStack() as ph:
                    xs = T(ph, nc, 'cx', [128, NT, 512], BF16)
                    ss_ = T(ph, nc, 'cs', [128, NT, 512], BF16)
                    dd_ = T(ph, nc, 'cd', [128, NT, 512], BF16)
                    fre = [T(ph, nc, 'fre%d' % i, [128, NT, 128], BF16) for i in range(2)]
                    fim = [T(ph, nc, 'fim%d' % i, [128, NT, 128], BF16) for i in range(2)]
                    ksb = [T(ph, nc, 'ksb%d' % i, [128, 512], F32) for i in range(2)]
                    tmp = [T(ph, nc, 'ctmp%d' % i, [128, 512], F32) for i in range(2)]
                    zst = [T(ph, nc, 'zst%d' % i, [128, 2, 512], BF16) for i in range(2)]
                    src = hy_scr[2] if o == 0 else y1_scr
                    srcb = hy_b if o == 0 else y1_b
                    xstg = [T(ph, nc, 'xstg%d' % i, [128, 4, 512], F32) for i in range(2)]
                    for q_ in range(NT // 4):
                        sg = xstg[q_ % 2]
                        ld(sg, sg.t[:], src[q_ * 512:(q_ + 1) * 512, :].rearrange("(k p) c -> p k c", p=128), rb=(srcb,))
                        op('dve', lambda e, sg=sg, q_=q_: e.tensor_copy(out=xs.t[:, q_ * 4:(q_ + 1) * 4, :], in_=sg.t[:]), r=(sg.b,), w=(xs.b,))
                    ld(ss_, ss_.t[:], filt_scr[2 * o].rearrange("(k p) c -> p k c", p=128), rb=(filt_b,))
                    ld(dd_, dd_.t[:], filt_scr[2 * o + 1].rearrange("(k p) c -> p k c", p=128), rb=(filt_b,))
                    for i in range(NF + 1):
                        fr = fre[i % 2]; fi = fim[i % 2]; zs = zst[i % 2]
                        ld(fr, fr.t[:], c_Fre[i])
                        pxr, pxrb = psget()
                        pkr, pkrb = psget()

                        def f(e, fr=fr, pxr=pxr, pkr=pkr):
                            for k in range(NT):
                                e.matmul(pxr[:, :], fr.t[:, k, :], xs.t[:, k, :], start=(k == 0), stop=(k == NT - 1))
                            for k in range(NT):
                                ins = e.matmul(pkr[:, :], fr.t[:, k, :], ss_.t[:, k, :], start=(k == 0), stop=(k == NT - 1))
                            return ins
                        op('pe', f, r=(fr.b, xs.b, ss_.b), w=(pxrb, pkrb))
                        op('act', lambda e, pkr=pkr: e.copy(out=ksb[0].t[:], in_=pkr[:, :]), r=(pkrb,), w=(ksb[0].b,))
                        if i < NF:
                            ld(fi, fi.t[:], c_Fim[i])
                            pxi, pxib = psget()
                            pki, pkib = psget()

                            def f(e, fi=fi, pxi=pxi, pki=pki):
                                for k in range(NT):
                                    e.matmul(pxi[:, :], fi.t[:, k, :], xs.t[:, k, :], start=(k == 0), stop=(k == NT - 1))
                                for k in range(NT):
                                    ins = e.matmul(pki[:, :], fi.t[:, k, :], dd_.t[:, k, :], start=(k == 0), stop=(k == NT - 1))
                                return ins
                            op('pe', f, r=(fi.b, xs.b, dd_.b), w=(pxib, pkib))
                            op('act', lambda e, pki=pki: e.copy(out=ksb[1].t[:], in_=pki[:, :]), r=(pkib,), w=(ksb[1].b,))
                            op('dve', lambda e, pxr=pxr: e.tensor_tensor(out=tmp[0].t[:], in0=pxr[:, :], in1=ksb[0].t[:], op=ALU.mult), r=(pxrb, ksb[0].b), w=(tmp[0].b,))
                            op('dve', lambda e, pxi=pxi: e.tensor_tensor(out=tmp[1].t[:], in0=pxi[:, :], in1=ksb[1].t[:], op=ALU.mult), r=(pxib, ksb[1].b), w=(tmp[1].b,))
                            op('dve', lambda e, zs=zs: e.tensor_tensor(out=zs.t[:, 0, :], in0=tmp[0].t[:], in1=tmp[1].t[:], op=ALU.subtract),
                               r=(tmp[0].b, tmp[1].b), w=(zs.b,))
                            op('dve', lambda e, pxr=pxr: e.tensor_tensor(out=tmp[0].t[:], in0=pxr[:, :], in1=ksb[1].t[:], op=ALU.mult), r=(pxrb, ksb[1].b), w=(tmp[0].b,))
                            op('dve', lambda e, pxi=pxi: e.tensor_tensor(out=tmp[1].t[:], in0=pxi[:, :], in1=ksb[0].t[:], op=ALU.mult), r=(pxib, ksb[0].b), w=(tmp[1].b,))
                            op('dve', lambda e, zs=zs: e.tensor_tensor(out=zs.t[:, 1, :], in0=tmp[0].t[:], in1=tmp[1].t[:], op=ALU.add),
                               r=(tmp[0].b, tmp[1].b), w=(zs.b,))
                            st(zs, zs.t[:], z_scr[2 * i:2 * i + 2].rearrange("z p c -> p z c"), wb=(z_b,))
                        else:
                            op('dve', lambda e, pxr=pxr, zs=zs: e.tensor_tensor(out=zs.t[:, 0, :], in0=pxr[:, :], in1=ksb[0].t[:], op=ALU.mult),
                               r=(pxrb, ksb[0].b), w=(zs.b,))
                            st(zs, zs.t[:, 0, :], z_scr[2 * i], wb=(z_b,))
                    P.barrier()
                with ExitStack() as ph:
                    zz = T(ph, nc, 'zz', [128, 2 * NF + 1, 512], BF16)
                    ire = [T(ph, nc, 'ire%d' % i, [128, NF + 1, 128], BF16) for i in range(2)]
                    iim = [T(ph, nc, 'iim%d' % i, [128, NF, 128], BF16) for i in range(2)]
                    skbc = T(ph, nc, 'skbc', [128, 512], F32)
                    gnbc = T(ph, nc, 'gnbcB', [128, 512], F32)
                    ld(gnbc, gnbc.t[:], Wd['group_norm_g'][l][512:1024].partition_broadcast(128))
                    vt = [T(ph, nc, 'cvt%d' % i, [128, 512], F32) for i in range(2)]
                    mt = [T(ph, nc, 'cmt%d' % i, [128, 512], F32) for i in range(2)]
                    yo = [T(ph, nc, 'cyo%d' % i, [128, 512], F32) for i in range(2)]
                    sm = T(ph, nc, 'csm', [128, 4], F32)
                    junk = T(ph, nc, 'cjunk', [128, 512], BF16)
                    yst = [T(ph, nc, 'cyst%d' % i, [128, 4, 128], BF16) for i in range(2)]
                    ld(zz, zz.t[:], z_scr.rearrange("z p c -> p z c"), rb=(z_b,))
                    ld(skbc, skbc.t[:], Wd['hy_skip'][l][o].partition_broadcast(128))
                    src = hy_scr[2] if o == 0 else y1_scr
                    srcb = hy_b if o == 0 else y1_b
                    for tt in range(NT):
                        ir = ire[tt % 2]; ii = iim[tt % 2]; v_ = vt[tt % 2]; m_ = mt[tt % 2]; y_ = yo[tt % 2]
                        ld(ir, ir.t[:], c_Ire[tt])
                        ld(ii, ii.t[:], c_Iim[tt])
                        ld(v_, v_.t[:], src[tt * 128:(tt + 1) * 128, :], rb=(srcb,))
                        ld(m_, m_.t[:], hy_scr[o][tt * 128:(tt + 1) * 128, :], rb=(hy_b,))
                        ps, psb = psget()

                        def f(e, ps=ps, ir=ir, ii=ii):
                            for k in range(NF + 1):
                                e.matmul(ps[:, :], ir.t[:, k, :], zz.t[:, 2 * k, :], start=(k == 0), stop=False)
                            for k in range(NF):
                                ins = e.matmul(ps[:, :], ii.t[:, k, :], zz.t[:, 2 * k + 1, :], start=False, stop=(k == NF - 1))
                            return ins
                        op('pe', f, r=(ir.b, ii.b, zz.b), w=(psb,))
                        op('dve', lambda e, v_=v_: e.tensor_tensor(out=v_.t[:], in0=v_.t[:], in1=skbc.t[:], op=ALU.mult), r=(skbc.b,), w=(v_.b,))
                        op('dve', lambda e, v_=v_, ps=ps: e.tensor_tensor(out=v_.t[:], in0=ps[:, :], in1=v_.t[:], op=ALU.add), r=(psb,), w=(v_.b,))
                        op('dve', lambda e, v_=v_, m_=m_, y_=y_, tt=tt: e.scalar_tensor_tensor(out=y_.t[:], in0=v_.t[:], scalar=tmask.t[:, tt:tt + 1], in1=m_.t[:],
                                                                                             op0=ALU.mult, op1=ALU.mult), r=(v_.b, m_.b, tmask.b), w=(y_.b,))
                        if o == 0:
                            st(y_, y_.t[:], y1_scr[tt * 128:(tt + 1) * 128, :], wb=(y1_b,))
                        else:
                            op('dve', lambda e: e.memset(sm.t[:], 0.0), w=(sm.b,))
                            op('act', lambda e, y_=y_: e.activation(out=junk.t[:], in_=y_.t[:], func=AF.Square, accum_out=sm.t[:, 0:1]),
                               r=(y_.b,), w=(junk.b, sm.b))
                            op('act', lambda e: e.activation(out=sm.t[:, 1:2], in_=sm.t[:, 0:1], func=AF.Sqrt, scale=1.0 / 512, bias=cst.t[:, 0:1]),
                               r=(cst.b,), w=(sm.b,))
                            op('dve', lambda e: e.reciprocal(out=sm.t[:, 1:2], in_=sm.t[:, 1:2]), w=(sm.b,))
                            op('dve', lambda e, y_=y_: e.scalar_tensor_tensor(out=y_.t[:], in0=y_.t[:], scalar=sm.t[:, 1:2], in1=gnbc.t[:, 0:512],
                                                                              op0=ALU.mult, op1=ALU.mult), r=(sm.b, gnbc.b), w=(y_.b,))
                            pt, ptb = psget()

                            def f(e, pt=pt, y_=y_):
                                for j in range(4):
                                    ins = e.transpose(pt[:, j * 128:(j + 1) * 128], y_.t[:, j * 128:(j + 1) * 128], ident.t[:])
                                return ins
                            op('pe', f, r=(y_.b, ident.b), w=(ptb,))
                            ys = yst[tt % 2]
                            op('act', lambda e, pt=pt, ys=ys: e.copy(out=ys.t[:].rearrange("p c t -> p (c t)"), in_=pt[:, :]), r=(ptb,), w=(ys.b,))
                            st(ys, ys.t[:], ycat_scr[512:1024, tt * 128:(tt + 1) * 128].rearrange("(c p) t -> p c t", p=128), wb=(ycat_b,))
                    P.barrier()

            def attention(ph, head_fn, scale, kb, grp, tagp):
                pbuf = [T(ph, nc, tagp + 'pb%d' % i, [128, TB], BF16) for i in range(3)]
                rd = T(ph, nc, tagp + 'rd', [128, TB], F32)
                yh = [T(ph, nc, tagp + 'yh%d' % i, [128, TB], F32) for i in range(4)]
                sqs = [T(ph, nc, tagp + 'sq%d' % i, [128, TB], BF16) for i in range(4)]
                rs = T(ph, nc, tagp + 'rs', [128, TB], F32)
                yst = [T(ph, nc, tagp + 'yst%d' % i, [128, TB], BF16) for i in range(2)]
                gc = 52
                colvec(gc, Wd['group_norm_g'][l][grp * 512:(grp + 1) * 512], 512)
                pi = 0
                si = 0
                hc = 0
                LA = 2
                for qb in range(NB):
                    ps_allowed[0] = [3]
                    heads = head_fn(qb)
                    steps = [(hi, kt) for hi in range(len(heads)) for kt in range(NT)]
                    banks = {}
                    for hi in range(len(heads)):
                        ob = 4 + 2 * (hc % 2); hc += 1
                        banks[hi] = (PS[ob], PSB[ob], PS[ob + 1], PSB[ob + 1])
                    slots = {}

                    def qk(hi, kt):
                        nonlocal pi
                        hd = heads[hi]
                        ps, psb = PS[pi % 3], PSB[pi % 3]
                        pb = pbuf[pi % 3]; pi += 1
                        slots[(hi, kt)] = pb

                        def f(e, ps=ps, hd=hd, kt=kt):
                            n = len(hd['q'])
                            for i in range(n):
                                ins = e.matmul(ps[:, 0:TB], hd['k'][i][1](kt), hd['q'][i][1], start=(i == 0), stop=(i == n - 1))
                            return ins
                        op('pe', f, r=tuple(t_.b for t_, _ in hd['q']) + tuple(t_.b for t_, _ in hd['k']), w=(psb,))
                        op('act', lambda e, ps=ps, pb=pb, kt=kt: e.activation(out=pb.t[:, :], in_=ps[:, 0:TB], func=AF.Exp, scale=scale, bias=kb.t[:, kt:kt + 1]),
                           r=(psb, kb.b), w=(pb.b,))

                    for i in range(min(LA, len(steps))):
                        qk(*steps[i])
                    for i, (hi, kt) in enumerate(steps):
                        if i + LA < len(steps):
                            qk(*steps[i + LA])
                        hd = heads[hi]
                        po, pob, pd, pdb = banks[hi]
                        pb = slots.pop((hi, kt))

                        def f(e, pb=pb, hd=hd, kt=kt, po=po, pd=pd):
                            e.matmul(po[:, 0:TB], hd['v'][1](kt), pb.t[:, :], start=(kt == 0), stop=(kt == NT - 1))
                            return e.matmul(pd[:, 0:TB], ones.t[:, :], pb.t[:, :], start=(kt == 0), stop=(kt == NT - 1))
                        op('pe', f, r=(pb.b, hd['v'][0].b, ones.b), w=(pob, pdb))
                        if kt == NT - 1:
                            op('dve', lambda e, pd=pd: e.reciprocal(out=rd.t[:, :], in_=pd[:, 0:TB]), r=(pdb,), w=(rd.b,))
                            op('dve', lambda e, hi=hi, po=po: e.tensor_tensor(out=yh[hi].t[:, :], in0=po[:, 0:TB], in1=rd.t[:, :], op=ALU.mult), r=(pob, rd.b), w=(yh[hi].b,))
                    pn, pnb = PS[3], PSB[3]
                    fm_sumsq([(yh[i], yh[i].t[:, :]) for i in range(4)], TB, pn, pnb, sqs)
                    rstd_from(pn, pnb, TB, 512, rs)
                    for hi in range(4):
                        ys = yst[si % 2]; si += 1
                        op('dve', lambda e, hi=hi, ys=ys: e.scalar_tensor_tensor(out=ys.t[:, :], in0=yh[hi].t[:, :], scalar=gcol.t[:, gc + hi:gc + hi + 1], in1=rs.t[:, :],
                                                                                 op0=ALU.mult, op1=ALU.mult), r=(yh[hi].b, gcol.b, rs.b), w=(ys.b,))
                        st(ys, ys.t[:, :], ycat_scr[grp * 512 + hi * 128:grp * 512 + (hi + 1) * 128, qb * TB:(qb + 1) * TB], wb=(ycat_b,))
                ps_allowed[0] = list(range(8))

            def make_tabs(ph, tag, p, c_cos, c_sin):
                ct = [T(ph, nc, tag + 'ct%d' % i, [p, TB], F32) for i in range(2)]
                sn = [T(ph, nc, tag + 'sn%d' % i, [p, TB], F32) for i in range(2)]
                cnt = [0]

                def get(tb):
                    i = cnt[0] % 2
                    cnt[0] += 1
                    ld(ct[i], ct[i].t[:, :], c_cos[:, tb * TB:(tb + 1) * TB])
                    ld(sn[i], sn[i].t[:, :], c_sin[:, tb * TB:(tb + 1) * TB])
                    return ct[i], sn[i]
                return get

            def qk_prep(ph_tiles, src_parts, n, gcols, dim, dst_parts, rope):
                sqs, rs, qn, t1, t2 = ph_tiles
                pn, pnb = psget()
                fm_sumsq(src_parts, n, pn, pnb, sqs)
                yield
                rstd_from(pn, pnb, n, dim, rs)
                yield
                for i, (t_, ap) in enumerate(src_parts):
                    p = ap.shape[0]
                    dt_, dap = dst_parts[i]
                    if rope is not None and rope[3] == i:
                        op('dve', lambda e, ap=ap, p=p, i=i: e.scalar_tensor_tensor(out=qn.t[0:p, 0:n], in0=ap, scalar=gcol.t[0:p, gcols[i]:gcols[i] + 1], in1=rs.t[0:p, 0:n],
                                                                                    op0=ALU.mult, op1=ALU.mult), r=(t_.b, gcol.b, rs.b), w=(qn.b,))
                        pr, prb = psget()
                        op('pe', lambda e, pr=pr, p=p: e.matmul(pr[0:p, 0:n], rope[0].t[0:p, 0:p], qn.t[0:p, 0:n], start=True, stop=True), r=(rope[0].b, qn.b), w=(prb,))
                        op('dve', lambda e, p=p: e.tensor_tensor(out=t1.t[0:p, 0:n], in0=qn.t[0:p, 0:n], in1=rope[1].t[0:p, 0:n], op=ALU.mult), r=(qn.b, rope[1].b), w=(t1.b,))
                        yield
                        op('dve', lambda e, pr=pr, p=p: e.tensor_tensor(out=t2.t[0:p, 0:n], in0=pr[0:p, 0:n], in1=rope[2].t[0:p, 0:n], op=ALU.mult), r=(prb, rope[2].b), w=(t2.b,))
                        op('dve', lambda e, p=p, dap=dap: e.tensor_tensor(out=dap, in0=t1.t[0:p, 0:n], in1=t2.t[0:p, 0:n], op=ALU.add), r=(t1.b, t2.b), w=(dt_.b,))
                    else:
                        op('dve', lambda e, ap=ap, p=p, i=i, dap=dap: e.scalar_tensor_tensor(out=dap, in0=ap, scalar=gcol.t[0:p, gcols[i]:gcols[i] + 1], in1=rs.t[0:p, 0:n],
                                                                                             op0=ALU.mult, op1=ALU.mult), r=(t_.b, gcol.b, rs.b), w=(dt_.b,))

            def mk_tiles(ph, tag, nsq):
                return ([T(ph, nc, tag + 'sq%d' % i, [128, TB], BF16) for i in range(nsq)], T(ph, nc, tag + 'rs', [128, TB], F32),
                        T(ph, nc, tag + 'qn', [128, TB], BF16), T(ph, nc, tag + 't1', [128, TB], F32), T(ph, nc, tag + 't2', [128, TB], F32))

            with ExitStack() as ph:
                permG = T(ph, nc, 'permG', [128, 128], BF16)
                ld(permG, permG.t[:], c_permG)
                tabs = make_tabs(ph, 'g', 128, c_cosG, c_sinG)
                colvec(48, Wd['gqa_qn_g'][l], 128)
                colvec(49, Wd['gqa_kn_g'][l], 128)
                QT = [T(ph, nc, 'gQT%d' % i, [128, S], BF16) for i in range(4)]
                KT_ = [T(ph, nc, 'gKT%d' % i, [128, S], BF16) for i in range(2)]
                V_ = [T(ph, nc, 'gV%d' % i, [128, NT, 128], BF16) for i in range(2)]
                rowb = [T(ph, nc, 'grow%d' % i, [128, S], F32) for i in range(2)]
                tsets = [mk_tiles(ph, 'ga', 1), mk_tiles(ph, 'gb', 1)]

                def gq_gen(hh, tb, idx):
                    row = rowb[hh % 2]
                    if tb == 0:
                        ld(row, row.t[:], proj_scr[2560 + hh * 128:2560 + (hh + 1) * 128, :], rb=(proj_b[20 + hh],))
                    dst = QT[hh] if hh < 4 else KT_[hh - 4]
                    sl = slice(tb * TB, (tb + 1) * TB)
                    ct_, sn_ = tabs(tb)
                    yield from qk_prep(tsets[idx % 2], [(row, row.t[:, sl])], TB, [48 if hh < 4 else 49], 128, [(dst, dst.t[:, sl])], (permG, ct_, sn_, 0))
                run_pipelined((gq_gen(hh, tb, hh * NB + tb) for hh in range(6) for tb in range(NB)), 2)
                for j in range(2):
                    row = rowb[j % 2]
                    ld(row, row.t[:], proj_scr[3328 + j * 128:3328 + (j + 1) * 128, :], rb=(proj_b[26 + j],))
                    for tq in range(NT // 4):
                        ps, psb = psget()

                        def f(e, ps=ps, tq=tq, row=row):
                            for jj in range(4):
                                tt = tq * 4 + jj
                                ins = e.transpose(ps[:, jj * 128:(jj + 1) * 128], row.t[:, tt * 128:(tt + 1) * 128], ident.t[:])
                            return ins
                        op('pe', f, r=(row.b, ident.b), w=(psb,))
                        op('act', lambda e, ps=ps, j=j, tq=tq: e.copy(out=V_[j].t[:, tq * 4:(tq + 1) * 4, :].rearrange("p a c -> p (a c)"), in_=ps[:, :]), r=(psb,), w=(V_[j].b,))
                P.barrier()

                def gheads(qb):
                    hs_ = []
                    for h in range(4):
                        j = h // 2
                        hs_.append(dict(q=[(QT[h], QT[h].t[:, qb * TB:(qb + 1) * TB])],
                                        k=[(KT_[j], lambda kt, j=j: KT_[j].t[:, kt * 128:(kt + 1) * 128])],
                                        v=(V_[j], lambda kt, j=j: V_[j].t[:, kt, :])))
                    return hs_
                attention(ph, gheads, 1.0 / math.sqrt(128.0), kbG, 2, 'g')
                P.barrier()

            with ExitStack() as ph:
                permM = T(ph, nc, 'permM', [64, 64], BF16)
                ld(permM, permM.t[:], c_permM)
                tabs = make_tabs(ph, 'm', 64, c_cosM, c_sinM)
                colvec(40, Wd['mla_q_a_g'][l], 512)
                colvec(44, Wd['mla_kv_a_g'][l], 256)
                colvec(46, Wd['mla_qn_g'][l][0:128], 128)
                colvec(47, Wd['mla_qn_g'][l][128:192], 64)
                colvec(50, Wd['mla_kn_g'][l][0:128], 128)
                colvec(51, Wd['mla_kn_g'][l][128:192], 64)
                KnT = [T(ph, nc, 'mKn%d' % i, [128, S], BF16) for i in range(4)]
                KrT = [T(ph, nc, 'mKr%d' % i, [64, S], BF16) for i in range(4)]
                V_ = [T(ph, nc, 'mV%d' % i, [128, NT, 128], BF16) for i in range(4)]
                wq = T(ph, nc, 'mwq', [128, 4, 768], BF16)
                tiles = mk_tiles(ph, 'ma', 4)
                tsets = [tiles, mk_tiles(ph, 'mb', 2)]
                with ExitStack() as ph2:
                    ckvT = T(ph2, nc, 'ckvT', [128, 2, S], BF16)
                    krT = T(ph2, nc, 'krT', [64, S], F32)
                    wkv = T(ph2, nc, 'mwkv', [128, 2, 1024], BF16)
                    with ExitStack() as ph3:
                        wqs = T(ph3, nc, 'mwqs', [128, 4, 768], F32)
                        wkvs = T(ph3, nc, 'mwkvs', [128, 2, 1024], F32)
                        ld(wqs, wqs.t[:], Wd['mla_w_q_b'][l].rearrange("(k p) n -> p k n", p=128))
                        ld(wkvs, wkvs.t[:], Wd['mla_w_kv_b'][l].rearrange("(k p) n -> p k n", p=128))
                        op('dve', lambda e: e.tensor_copy(out=wq.t[:], in_=wqs.t[:]), r=(wqs.b,), w=(wq.b,))
                        op('dve', lambda e: e.tensor_copy(out=wkv.t[:], in_=wkvs.t[:]), r=(wkvs.b,), w=(wkv.b,))
                        P.barrier()
                    ld(krT, krT.t[:], proj_scr[4352:4416, :], rb=(proj_b[34],))
                    rq = [T(ph2, nc, 'mrq%d' % i, [128, 2, TB], F32) for i in range(2)]
                    kn_f2 = [T(ph2, nc, 'mknf%d' % i, [128, TB], F32) for i in range(2)]

                    def mk_gen(tb):
                        tl = tsets[tb % 2]
                        kn_f = kn_f2[tb % 2]
                        sl = slice(tb * TB, (tb + 1) * TB)
                        r2 = rq[tb % 2]
                        ld(r2, r2.t[:], proj_scr[4096:4352, sl].rearrange("(c p) t -> p c t", p=128), rb=tuple(proj_b[32:34]))
                        yield from qk_prep(tl, [(r2, r2.t[:, c, :]) for c in range(2)], TB, [44, 45], 256,
                                           [(ckvT, ckvT.t[:, c, sl]) for c in range(2)], None)
                        ct_, sn_ = tabs(tb)
                        for h in range(4):
                            yield
                            ps, psb = psget()
                            op('pe', lambda e, ps=ps, h=h, sl=sl: [e.matmul(ps[:, 0:TB], wkv.t[:, k, h * 256:h * 256 + 128], ckvT.t[:, k, sl], start=(k == 0), stop=(k == 1))
                                                                   for k in range(2)][-1], r=(wkv.b, ckvT.b), w=(psb,))
                            op('act', lambda e, ps=ps: e.copy(out=kn_f.t[:, :], in_=ps[:, 0:TB]), r=(psb,), w=(kn_f.b,))
                            for jj in range(TPB):
                                tt = tb * TPB + jj
                                ps, psb = psget()
                                op('pe', lambda e, ps=ps, h=h, tt=tt: [e.matmul(ps[:, 0:128], ckvT.t[:, k, tt * 128:(tt + 1) * 128], wkv.t[:, k, h * 256 + 128:h * 256 + 256],
                                                                                  start=(k == 0), stop=(k == 1)) for k in range(2)][-1], r=(wkv.b, ckvT.b), w=(psb,))
                                op('act', lambda e, ps=ps, h=h, tt=tt: e.copy(out=V_[h].t[:, tt, :], in_=ps[:, 0:128]), r=(psb,), w=(V_[h].b,))
                            yield from qk_prep(tl, [(kn_f, kn_f.t[:, :]), (krT, krT.t[:, sl])], TB, [50, 51], 192,
                                               [(KnT[h], KnT[h].t[:, sl]), (KrT[h], KrT[h].t[:, sl])], (permM, ct_, sn_, 1))
                    run_pipelined((mk_gen(tb) for tb in range(NB)), 2)
                    P.barrier()
                rq4 = [T(ph, nc, 'mrq4%d' % i, [128, 4, TB], F32) for i in range(2)]
                qan = T(ph, nc, 'mqan', [128, 4, TB], BF16)
                Qn = [T(ph, nc, 'mQn%d' % i, [128, TB], BF16) for i in range(4)]
                Qr = [T(ph, nc, 'mQr%d' % i, [64, TB], BF16) for i in range(4)]
                qf = T(ph, nc, 'mqf', [128, TB], F32)
                qrf = T(ph, nc, 'mqrf', [64, TB], F32)

                def mheads(qb):
                    sl = slice(qb * TB, (qb + 1) * TB)
                    r_ = rq4[qb % 2]
                    ld(r_, r_.t[:], proj_scr[3584:4096, sl].rearrange("(c p) t -> p c t", p=128), rb=tuple(proj_b[28:32]))
                    for _ in qk_prep(tiles, [(r_, r_.t[:, c, :]) for c in range(4)], TB, [40, 41, 42, 43], 512,
                                     [(qan, qan.t[:, c, :]) for c in range(4)], None):
                        pass
                    ct_, sn_ = tabs(qb)
                    hs_ = []
                    for h in range(4):
                        ps, psb = psget()
                        op('pe', lambda e, ps=ps, h=h: [e.matmul(ps[:, 0:TB], wq.t[:, k, h * 192:h * 192 + 128], qan.t[:, k, :], start=(k == 0), stop=(k == 3))
                                                        for k in range(4)][-1], r=(wq.b, qan.b), w=(psb,))
                        op('act', lambda e, ps=ps: e.copy(out=qf.t[:, :], in_=ps[:, 0:TB]), r=(psb,), w=(qf.b,))
                        ps, psb = psget()
                        op('pe', lambda e, ps=ps, h=h: [e.matmul(ps[0:64, 0:TB], wq.t[:, k, h * 192 + 128:h * 192 + 192], qan.t[:, k, :], start=(k == 0), stop=(k == 3))
                                                        for k in range(4)][-1], r=(wq.b, qan.b), w=(psb,))
                        op('act', lambda e, ps=ps: e.copy(out=qrf.t[:, :], in_=ps[0:64, 0:TB]), r=(psb,), w=(qrf.b,))
                        for _ in qk_prep(tiles, [(qf, qf.t[:, :]), (qrf, qrf.t[:, :])], TB, [46, 47], 192,
                                         [(Qn[h], Qn[h].t[:, :]), (Qr[h], Qr[h].t[:, :])], (permM, ct_, sn_, 1)):
                            pass
                        hs_.append(dict(q=[(Qn[h], Qn[h].t[:, :]), (Qr[h], Qr[h].t[:, :])],
                                        k=[(KnT[h], lambda kt, h=h: KnT[h].t[:, kt * 128:(kt + 1) * 128]), (KrT[h], lambda kt, h=h: KrT[h].t[:, kt * 128:(kt + 1) * 128])],
                                        v=(V_[h], lambda kt, h=h: V_[h].t[:, kt, :])))
                    return hs_
                attention(ph, mheads, 1.0 / math.sqrt(192.0), kbM, 3, 'm')
                P.barrier()

            with ExitStack() as ph:
                wo = T(ph, nc, 'wo', [128, KC, D], BF16)
                g1bc = T(ph, nc, 'g1bc', [128, D], F32)
                ld(g1bc, g1bc.t[:], mod_scr[l][2 * D:3 * D].partition_broadcast(128), rb=(mod_b,))
                for k4 in range(4):
                    ld(wo, wo.t[:, k4 * 4:(k4 + 1) * 4, :], wbf['w_out'][l][k4 * 512:(k4 + 1) * 512, :].rearrange("(k p) n -> p k n", p=128), rb=tuple(wbf_b['w_out'][l]))
                yc = [T(ph, nc, 'oyc%d' % i, [128, KC, TB], BF16) for i in range(2)]
                xt = [T(ph, nc, 'oxt%d' % i, [128, D], F32) for i in range(2)]
                tm_ = T(ph, nc, 'otm', [128, 512], F32)
                for tb in range(NB):
                    yt = yc[tb % 2]
                    ld(yt, yt.t[:], ycat_scr[:, tb * TB:(tb + 1) * TB].rearrange("(k p) t -> p k t", p=128), rb=(ycat_b,))
                    for jj in range(TPB):
                        tt = tb * TPB + jj
                        x = xt[tt % 2]
                        ld(x, x.t[:], (x_in if first else y_out)[tt * 128:(tt + 1) * 128, :], rb=(res_b[tt],))
                        for nb in range(4):
                            ps, psb = psget()

                            def f(e, ps=ps, yt=yt, jj=jj, nb=nb):
                                for k in range(KC):
                                    ins = e.matmul(ps[:, :], yt.t[:, k, jj * 128:(jj + 1) * 128], wo.t[:, k, nb * 512:(nb + 1) * 512], start=(k == 0), stop=(k == KC - 1))
                                return ins
                            op('pe', f, r=(yt.b, wo.b), w=(psb,))
                            op('dve', lambda e, ps=ps, nb=nb: e.tensor_tensor(out=tm_.t[:], in0=ps[:, :], in1=g1bc.t[:, nb * 512:(nb + 1) * 512], op=ALU.mult),
                               r=(psb, g1bc.b), w=(tm_.b,))
                            op('dve', lambda e, x=x, nb=nb: e.tensor_tensor(out=x.t[:, nb * 512:(nb + 1) * 512], in0=x.t[:, nb * 512:(nb + 1) * 512], in1=tm_.t[:], op=ALU.add),
                               r=(tm_.b,), w=(x.b,))
                        st(x, x.t[:], y_out[tt * 128:(tt + 1) * 128, :], wb=(res_b[tt],))
                P.barrier()

            for hb in range(2):
                t0 = hb * FC
                with ExitStack() as ph:
                    h2T = T(ph, nc, 'h2T', [128, KC, FC + 2], BF16)
                    op('dve', lambda e: e.memset(h2T.t[:, :, 0:1], 0.0), w=(h2T.b,))
                    op('dve', lambda e: e.memset(h2T.t[:, :, FC + 1:FC + 2], 0.0), w=(h2T.b,))
                    lst = []
                    if t0 > 0:
                        lst.append((t0 // 128 - 1, 0, 127, 1))
                    for tt in range(t0 // 128, (t0 + FC) // 128):
                        lst.append((tt, 1 + (tt * 128 - t0), 0, 128))
                    if t0 + FC < S:
                        lst.append(((t0 + FC) // 128, FC + 1, 0, 1))
                    with ExitStack() as ph2:
                        norm_phase(ph2, l, 1, h2T, lst, False)
                        P.barrier()
                    cw_ = T(ph, nc, 'fcw', [128, 4, 88], F32)
                    for i in range(3):
                        ld(cw_, cw_.t[:, i, :], Wd['ffn_conv_w'][l][i].rearrange("(c p) -> p c", p=128), slow=True)
                    ld(cw_, cw_.t[:, 3, :], Wd['ffn_conv_b'][l].rearrange("(c p) -> p c", p=128), slow=True)
                    wbuf = [T(ph, nc, 'fwb%d' % i, [128, KC, 2, 256], BF16) for i in range(2)]
                    raw2 = [[T(ph, nc, 'fraw%d_%d' % (i, q), [128, FC + 2], F32) for i in range(2)] for q in range(2)]
                    acc2 = [[T(ph, nc, 'facc%d_%d' % (i, q), [128, FC], F32) for i in range(2)] for q in range(2)]
                    gst = [T(ph, nc, 'fgst%d' % i, [128, FC], BF16) for i in range(2)]
                    blocks = [(c, min(512, FC + 2 - c)) for c in range(0, FC + 2, 512)]
                    def stage1(j):
                        wt = wbuf[(j // 2) % 2]
                        raw = raw2[j % 2]
                        wo_ = (j % 2) * 128
                        if j % 2 == 0:
                            ld(wt, wt.t[:, :, 0, :], wbf['ffn_w_up'][l][:, j * 128:(j + 2) * 128].rearrange("(k p) n -> p k n", p=128), rb=tuple(wbf_b['ffn_w_up'][l]))
                            ld(wt, wt.t[:, :, 1, :], wbf['ffn_w_up'][l][:, DFF + j * 128:DFF + (j + 2) * 128].rearrange("(k p) n -> p k n", p=128), rb=tuple(wbf_b['ffn_w_up'][l]))
                        for gi in range(2):
                            for (c0, n) in blocks:
                                ps, psb = psget()

                                def f(e, ps=ps, wt=wt, gi=gi, c0=c0, n=n, wo_=wo_):
                                    for k in range(KC):
                                        ins = e.matmul(ps[:, 0:n], wt.t[:, k, gi, wo_:wo_ + 128], h2T.t[:, k, c0:c0 + n], start=(k == 0), stop=(k == KC - 1))
                                    return ins
                                op('pe', f, r=(wt.b, h2T.b), w=(psb,))
                                op('act', lambda e, ps=ps, gi=gi, c0=c0, n=n, raw=raw: e.copy(out=raw[gi].t[:, c0:c0 + n], in_=ps[:, 0:n]), r=(psb,), w=(raw[gi].b,))

                    def stage2(j):
                        raw = raw2[j % 2]; acc = acc2[j % 2]
                        for gi in range(2):
                            ci = gi * 44 + j
                            if hb == 0:
                                op('dve', lambda e, gi=gi, raw=raw: e.tensor_scalar(out=raw[gi].t[:, FC + 1:FC + 2], in0=raw[gi].t[:, FC + 1:FC + 2], scalar1=mcol.t[:, 0:1],
                                                                                    scalar2=None, op0=ALU.mult), r=(mcol.b,), w=(raw[gi].b,))
                            op('dve', lambda e, gi=gi, ci=ci, raw=raw, acc=acc: e.tensor_scalar(out=acc[gi].t[:], in0=raw[gi].t[:, 1:FC + 1], scalar1=cw_.t[:, 1, ci:ci + 1],
                                                                                                scalar2=cw_.t[:, 3, ci:ci + 1], op0=ALU.mult, op1=ALU.add), r=(raw[gi].b, cw_.b), w=(acc[gi].b,))
                            op('dve', lambda e, gi=gi, ci=ci, raw=raw, acc=acc: e.scalar_tensor_tensor(out=acc[gi].t[:], in0=raw[gi].t[:, 0:FC], scalar=cw_.t[:, 0, ci:ci + 1], in1=acc[gi].t[:],
                                                                                                       op0=ALU.mult, op1=ALU.add), r=(raw[gi].b, cw_.b), w=(acc[gi].b,))
                            op('dve', lambda e, gi=gi, ci=ci, raw=raw, acc=acc: e.scalar_tensor_tensor(out=acc[gi].t[:], in0=raw[gi].t[:, 2:FC + 2], scalar=cw_.t[:, 2, ci:ci + 1], in1=acc[gi].t[:],
                                                                                                       op0=ALU.mult, op1=ALU.add), r=(raw[gi].b, cw_.b), w=(acc[gi].b,))
                        op('act', lambda e, acc=acc: e.activation(out=acc[0].t[:], in_=acc[0].t[:], func=AF.Silu), w=(acc[0].b,))
                        gs_ = gst[j % 2]
                        op('dve', lambda e, gs_=gs_, acc=acc: e.tensor_tensor(out=gs_.t[:], in0=acc[0].t[:], in1=acc[1].t[:], op=ALU.mult), r=(acc[0].b, acc[1].b), w=(gs_.b,))
                        st(gs_, gs_.t[:].rearrange("p (b t) -> p b t", t=GB), g_scr[t0 // GB:(t0 + FC) // GB, :, j, :].rearrange("b p t -> p b t"), wb=(g_b,))

                    for j in range(45):
                        if j < 44:
                            stage1(j)
                        if j >= 1:
                            stage2(j - 1)
                    P.barrier()

            with ExitStack() as ph:
                wd_ = T(ph, nc, 'wdn', [128, 44, 512], BF16)
                g2bc = T(ph, nc, 'g2bc', [128, D], F32)
                ld(g2bc, g2bc.t[:], mod_scr[l][5 * D:6 * D].partition_broadcast(128), rb=(mod_b,))
                gt = [T(ph, nc, 'dgt%d' % i, [128, 44, GB], BF16) for i in range(2)]
                xt = [T(ph, nc, 'dxt%d' % i, [128, 512], F32) for i in range(2)]
                tm_ = T(ph, nc, 'dtm', [128, 512], F32)
                gi_ = 0
                for fb in range(4):
                    for k4 in range(4):
                        ld(wd_, wd_.t[:, k4 * 11:(k4 + 1) * 11, :], wbf['ffn_w_down'][l][k4 * 11 * 128:(k4 + 1) * 11 * 128, fb * 512:(fb + 1) * 512].rearrange("(k p) n -> p k n", p=128), rb=tuple(wbf_b['ffn_w_down'][l]))
                    for tb in range(S // GB):
                        g_ = gt[gi_ % 2]; gi_ += 1
                        ld(g_, g_.t[:], g_scr[tb], rb=(g_b,))
                        for jj in range(GB // 128):
                            tt = tb * (GB // 128) + jj
                            x = xt[tt % 2]
                            ld(x, x.t[:], y_out[tt * 128:(tt + 1) * 128, fb * 512:(fb + 1) * 512], rb=(res_b[tt],))
                            ps, psb = psget()

                            def f(e, ps=ps, g_=g_, jj=jj):
                                for k in range(44):
                                    ins = e.matmul(ps[:, :], g_.t[:, k, jj * 128:(jj + 1) * 128], wd_.t[:, k, :], start=(k == 0), stop=(k == 43))
                                return ins
                            op('pe', f, r=(g_.b, wd_.b), w=(psb,))
                            op('dve', lambda e, ps=ps, fb=fb: e.tensor_tensor(out=tm_.t[:], in0=ps[:, :], in1=g2bc.t[:, fb * 512:(fb + 1) * 512], op=ALU.mult),
                               r=(psb, g2bc.b), w=(tm_.b,))
                            op('dve', lambda e, x=x: e.tensor_tensor(out=x.t[:], in0=x.t[:], in1=tm_.t[:], op=ALU.add), r=(tm_.b,), w=(x.b,))
                            st(x, x.t[:], y_out[tt * 128:(tt + 1) * 128, fb * 512:(fb + 1) * 512], wb=(res_b[tt],))
                P.barrier()
            print('layer', l, 'engine op counts', dict(P.cnt), 'dma sems', len(P.all_ds), max(d[1] for d in P.all_ds))
            P.rotate()
        P.barrier()
    return nc


def _consts(S, L):
    bf = ml_dtypes.bfloat16
    NT = S // 128
    N = 2 * S
    c = {}
    c['k_ident'] = np.eye(128, dtype=np.float32)
    c['k_ones'] = np.ones((128, 128), dtype=bf)

    def perm(n, half):
        m = np.zeros((n, n), np.float32)
        for d in range(n):
            p = d + half if (d % (2 * half)) < half else d - half
            m[p, d] = 1.0
        return m.astype(bf)
    c['k_permG'] = perm(128, 32)
    c['k_permM'] = perm(64, 16)
    t = np.arange(S)
    row = (t // 64).astype(np.float32)
    col = (t % 64).astype(np.float32)

    def ropetab(n):
        sec = n // 2
        half = sec // 2
        inv = (10000.0 ** (-np.arange(0, sec, 2, dtype=np.float32) / sec)).astype(np.float32)
        cs = np.zeros((n, S), np.float32)
        sn = np.zeros((n, S), np.float32)
        for d in range(n):
            pos = row if d < sec else col
            i = (d % sec) % half
            ang = (pos * inv[i]).astype(np.float32)
            cs[d] = np.cos(ang)
            sn[d] = np.sin(ang) * (-1.0 if (d % sec) < half else 1.0)
        return cs, sn
    c['k_cosG'], c['k_sinG'] = ropetab(128)
    c['k_cosM'], c['k_sinM'] = ropetab(64)
    pos = np.arange(L, dtype=np.float32)
    tt = np.linspace(0.0, 1.0, L, dtype=np.float32)
    bands = np.linspace(1e-4, 15, 16, dtype=np.float32)
    ang = (np.float32(2.0 * math.pi / L) * pos[:, None] * bands[None, :]).astype(np.float32)
    feats = np.concatenate([tt[:, None], np.cos(ang), -np.sin(ang)], axis=-1).astype(np.float32)
    fT = np.zeros((33, S), np.float32)
    fT[:, :L] = feats.T
    c['k_featsT'] = fT
    ntau = np.full((S,), -1e4, np.float32)
    ntau[:L] = -tt
    c['k_negtau'] = np.ascontiguousarray(ntau.reshape(NT, 128).T)
    km = np.full((S,), -30000.0, np.float32); km[:L] = 0.0
    c['k_keymask'] = np.ascontiguousarray(km.reshape(NT, 128).T)
    tm = np.zeros((S,), np.float32); tm[:L] = 1.0
    c['k_tokmask'] = np.ascontiguousarray(tm.reshape(NT, 128).T)
    c['k_mcol'] = np.full((128, 1), 1.0 if L == S else 0.0, np.float32)
    tt_ = np.arange(S, dtype=np.int64)
    f_re = np.arange((NT + 1) * 128, dtype=np.int64)
    ph = (tt_[:, None] * f_re[None, :]) % N
    Fre = np.cos(2.0 * np.pi * ph / N)
    Fre[:, S + 1:] = 0.0
    f_im = np.arange(S, dtype=np.int64)
    ph2 = (tt_[:, None] * f_im[None, :]) % N
    Fim = -np.sin(2.0 * np.pi * ph2 / N)
    c['k_Fre'] = np.ascontiguousarray(Fre.reshape(NT, 128, NT + 1, 128).transpose(2, 1, 0, 3)).astype(bf)
    c['k_Fim'] = np.ascontiguousarray(Fim.reshape(NT, 128, NT, 128).transpose(2, 1, 0, 3)).astype(bf)
    wre = np.full((NT + 1) * 128, 2.0 / N); wre[0] = 1.0 / N; wre[S] = 1.0 / N; wre[S + 1:] = 0.0
    Ire = (np.cos(2.0 * np.pi * ph / N) * wre[None, :]).T
    Iim = (-np.sin(2.0 * np.pi * ph2 / N) * (2.0 / N)).T
    c['k_Ire'] = np.ascontiguousarray(Ire.reshape(NT + 1, 128, NT, 128).transpose(2, 1, 0, 3)).astype(bf)
    c['k_Iim'] = np.ascontiguousarray(Iim.reshape(NT, 128, NT, 128).transpose(2, 1, 0, 3)).astype(bf)
    return c


_CACHE = {}


def run(x_list, c_list, L_list, weights, S, DEPTH):
    key = (S, DEPTH)
    if key not in _CACHE:
        _CACHE[key] = build(S, DEPTH)
    nc = _CACHE[key]
    cc = {}
    in_maps = []
    for x, cvec, L in zip(x_list, c_list, L_list):
        if L not in cc:
            cc[L] = _consts(S, L)
        m = dict(cc[L])
        xp = np.zeros((S, D), np.float32)
        xp[:L] = x
        m['x'] = xp
        m['c'] = np.ascontiguousarray(cvec, dtype=np.float32)
        for n in WEIGHT_NAMES:
            m[n] = weights[n]
        in_maps.append(m)
    res = run_bass_kernel_spmd(nc, in_maps, core_ids=list(range(len(in_maps))))
    return [r['y'] for r in res.results]


def kernel(**inputs):
    S = 4096
    weights = {n: np.ascontiguousarray(np.asarray(inputs[n], dtype=np.float32)) for n in WEIGHT_NAMES}
    xp = np.asarray(inputs['x_prompt'], dtype=np.float32)
    xs = np.asarray(inputs['x_sample'], dtype=np.float32)
    cp = np.asarray(inputs['c_prompt'], dtype=np.float32)
    cs = np.asarray(inputs['c_sample'], dtype=np.float32)
    xl = [xp[i] for i in range(4)] + [xs[i] for i in range(4)]
    cl = [cp[i] for i in range(4)] + [cs[i] for i in range(4)]
    Ls = [2048] * 4 + [4096] * 4
    ys = run(xl, cl, Ls, weights, S, 4)
    y_prompt = np.stack([ys[i][:2048] for i in range(4)], axis=0).astype(np.float32)
    y_sample = np.stack([ys[4 + i] for i in range(4)], axis=0).astype(np.float32)
    return (y_prompt, y_sample)
```

```python
import math
from contextlib import ExitStack
import numpy as np
import ml_dtypes
import concourse.bass as bass
import concourse.mybir as mybir
from concourse.bass_utils import run_bass_kernel_spmd

F32 = mybir.dt.float32
BF16 = mybir.dt.bfloat16
AF = mybir.ActivationFunctionType
ALU = mybir.AluOpType
AX = mybir.AxisListType

D = 2048
KC = 16
IN_COLS = 4416
DFF = 5632
EPS = 1e-6
SAME_ENG_SYNC = True
WEIGHT_NAMES = ['ada_w', 'ada_b', 'norm1_g', 'w_in', 'gm_vnorm_g', 'gm_spatial_w', 'gm_spatial_b', 'hy_conv_w',
                'hy_conv_b', 'hy_w1', 'hy_b1', 'hy_f1', 'hy_w2', 'hy_b2', 'hy_f2', 'hy_w3', 'hy_decay', 'hy_skip',
                'gqa_qn_g', 'gqa_kn_g', 'mla_q_a_g', 'mla_w_q_b', 'mla_kv_a_g', 'mla_w_kv_b', 'mla_qn_g',
                'mla_kn_g', 'group_norm_g', 'w_out', 'norm2_g', 'ffn_w_up', 'ffn_conv_w', 'ffn_conv_b', 'ffn_w_down']


class B:
    __slots__ = ('w', 'r', 'dsem', 'dcnt', 'bg')

    def __init__(self, bg=False):
        self.w = None
        self.r = {}
        self.dsem = None
        self.dcnt = 0
        self.bg = bg


class Prog:
    def __init__(self, nc, es):
        self.nc = nc
        self.es = es
        self.eng = {'pe': nc.tensor, 'act': nc.scalar, 'dve': nc.vector, 'pool': nc.gpsimd, 'sp': nc.sync}
        self.sem = {k: es.enter_context(nc.semaphore('sem_' + k)) for k in ('pe', 'act', 'dve', 'pool')}
        self.cnt = {k: 0 for k in self.sem}
        self.waited = {k: {} for k in self.eng}
        self.dsems = []
        self.free_ds = []
        self.old_sems = []
        self.all_ds = []
        self.nsem = 0

    def _wait(self, eng, sem, val):
        w = self.waited[eng]
        if w.get(id(sem), 0) < val:
            self.eng[eng].wait_ge(sem, val)
            w[id(sem)] = val

    def op(self, eng, fn, r=(), w=(), dma=None):
        deps = {}
        for b in r:
            if b.w is not None:
                s, v = b.w
                if deps.get(id(s), (None, 0))[1] < v:
                    deps[id(s)] = (s, v)
        for b in w:
            if b.w is not None:
                s, v = b.w
                if deps.get(id(s), (None, 0))[1] < v:
                    deps[id(s)] = (s, v)
            for s, v in b.r.values():
                if deps.get(id(s), (None, 0))[1] < v:
                    deps[id(s)] = (s, v)
        for s, v in deps.values():
            if eng == 'pe' and s is self.sem['pe']:
                continue
            if not SAME_ENG_SYNC and eng in self.sem and s is self.sem[eng]:
                continue
            self._wait(eng, s, v)
        ins = fn(self.eng[eng])
        if dma is None:
            self.cnt[eng] += 1
            tok = (self.sem[eng], self.cnt[eng])
            ins.then_inc(self.sem[eng], 1)
        else:
            if dma.dsem is None and dma.bg:
                dma.dsem = [self.es.enter_context(self.nc.semaphore('bg%d' % self.nsem)), 0]
                self.nsem += 1
            if dma.dsem is None:
                if self.free_ds:
                    dma.dsem = self.free_ds.pop()
                else:
                    dma.dsem = [self.es.enter_context(self.nc.semaphore('ds%d' % self.nsem)), 0]
                    self.nsem += 1
                    self.all_ds.append(dma.dsem)
                self.dsems.append(dma)
            dma.dsem[1] += 16
            tok = (dma.dsem[0], dma.dsem[1])
            ins.then_inc(dma.dsem[0], 16)
        for b in r:
            s, v = tok
            if b.r.get(id(s), (None, 0))[1] < v:
                b.r[id(s)] = tok
        for b in w:
            b.w = tok
            b.r = {}
        return tok

    def rotate(self):
        for k in list(self.sem):
            self.old_sems.append((self.sem[k], self.cnt[k]))
            self.sem[k] = self.es.enter_context(self.nc.semaphore('sem_%s_%d' % (k, len(self.old_sems))))
            self.cnt[k] = 0

    def barrier(self):
        for e in self.eng:
            for k in self.sem:
                if self.cnt[k]:
                    self._wait(e, self.sem[k], self.cnt[k])
            for d in self.all_ds:
                if d[1]:
                    self._wait(e, d[0], d[1])
        for b in self.dsems:
            self.free_ds.append(b.dsem)
            b.dsem = None
        self.dsems = []


_TN = [0]


def run_pipelined(gens, depth):
    it = iter(gens)
    active = []
    done = False
    while True:
        if not done and len(active) < depth:
            g = next(it, None)
            if g is None:
                done = True
            else:
                active.append(g)
        if not active:
            if done:
                break
            continue
        for g in list(active):
            try:
                next(g)
            except StopIteration:
                active.remove(g)


class T:
    def __init__(self, es, nc, name, shape, dt):
        _TN[0] += 1
        self.t = es.enter_context(nc.sbuf_tensor('%s_%d' % (name, _TN[0]), shape, dt))
        self.b = B()


def build(S, DEPTH):
    NT = S // 128
    TB = min(512, S)
    NB = S // TB
    TPB = TB // 128
    NF = NT
    FC = S // 2
    Lp = S // 2
    nc = bass.Bass("TRN2", target_bir_lowering=False)
    es = ExitStack()
    with es:
        P = Prog(nc, es)
        op = P.op

        def din(name, shape, dt=F32):
            return nc.dram_tensor(name, shape, dt, kind="ExternalInput").ap()

        def dscr(name, shape, dt=F32):
            return nc.dram_tensor(name, shape, dt, kind="Internal").ap()

        x_in = din('x', [S, D])
        c_in = din('c', [D])
        y_out = nc.dram_tensor('y', [S, D], F32, kind="ExternalOutput").ap()
        Wd = {}
        shapes = {'ada_w': [D, 6 * D], 'ada_b': [6 * D], 'norm1_g': [D], 'w_in': [D, IN_COLS], 'gm_vnorm_g': [512],
                  'gm_spatial_w': [4, 128, 128], 'gm_spatial_b': [4, 128], 'hy_conv_w': [3, 1536], 'hy_conv_b': [1536],
                  'hy_w1': [33, 64], 'hy_b1': [64], 'hy_f1': [64], 'hy_w2': [64, 64], 'hy_b2': [64], 'hy_f2': [64],
                  'hy_w3': [64, 2048], 'hy_decay': [2048], 'hy_skip': [2, 512], 'gqa_qn_g': [128], 'gqa_kn_g': [128],
                  'mla_q_a_g': [512], 'mla_w_q_b': [512, 768], 'mla_kv_a_g': [256], 'mla_w_kv_b': [256, 1024],
                  'mla_qn_g': [192], 'mla_kn_g': [192], 'group_norm_g': [D], 'w_out': [D, D], 'norm2_g': [D],
                  'ffn_w_up': [D, 2 * DFF], 'ffn_conv_w': [3, 2 * DFF], 'ffn_conv_b': [2 * DFF], 'ffn_w_down': [DFF, D]}
        for n in WEIGHT_NAMES:
            Wd[n] = din(n, [DEPTH] + shapes[n])
        c_ident = din('k_ident', [128, 128])
        c_ones = din('k_ones', [128, 128], BF16)
        c_permG = din('k_permG', [128, 128], BF16)
        c_permM = din('k_permM', [64, 64], BF16)
        c_cosG = din('k_cosG', [128, S]); c_sinG = din('k_sinG', [128, S])
        c_cosM = din('k_cosM', [64, S]); c_sinM = din('k_sinM', [64, S])
        c_feats = din('k_featsT', [33, S])
        c_ntau = din('k_negtau', [128, NT])
        c_kmask = din('k_keymask', [128, NT])
        c_tmask = din('k_tokmask', [128, NT])
        c_mcol = din('k_mcol', [128, 1])
        c_Fre = din('k_Fre', [NF + 1, 128, NT, 128], BF16)
        c_Fim = din('k_Fim', [NF, 128, NT, 128], BF16)
        c_Ire = din('k_Ire', [NT, 128, NF + 1, 128], BF16)
        c_Iim = din('k_Iim', [NT, 128, NF, 128], BF16)
        mod_scr = dscr('mod_scr', [DEPTH, 6 * D]); mod_b = B()
        proj_scr = dscr('proj_scr', [IN_COLS, S]); proj_b = [B() for _ in range(35)]
        ycat_scr = dscr('ycat_scr', [D, S], BF16); ycat_b = B()
        filt_scr = dscr('filt_scr', [4, S, 512], BF16); filt_b = B()
        hy_scr = dscr('hy_scr', [3, S, 512]); hy_b = B()
        y1_scr = dscr('y1_scr', [S, 512]); y1_b = B()
        z_scr = dscr('z_scr', [2 * NF + 1, 128, 512], BF16); z_b = B()
        GB = min(TB, FC)
        g_scr = dscr('g_scr', [DFF, S], BF16); g_b = B()
        res_b = [B() for _ in range(NT)]
        wbf = {}
        wbf_b = {}
        for n_, shp, npc in (('w_in', [D, IN_COLS], 2), ('w_out', [D, D], 1), ('ffn_w_up', [D, 2 * DFF], 4), ('ffn_w_down', [DFF, D], 4)):
            wbf[n_] = [dscr('%s_bf%d' % (n_, l_), shp, BF16) for l_ in range(DEPTH)]
            wbf_b[n_] = [[B(bg=True) for _ in range(npc)] for l_ in range(DEPTH)]

        PS = [es.enter_context(nc.psum_tensor('ps%d' % i, [128, 512], F32)) for i in range(8)]
        PSB = [B() for _ in range(8)]
        ident = T(es, nc, 'ident', [128, 128], F32)
        identb = T(es, nc, 'identb', [128, 128], BF16)
        ones = T(es, nc, 'ones', [128, 128], BF16)
        cst = T(es, nc, 'cst', [128, 8], F32)
        kmask = T(es, nc, 'kmask', [128, NT], F32)
        kbG = T(es, nc, 'kbG', [128, NT], F32)
        kbM = T(es, nc, 'kbM', [128, NT], F32)
        tmask = T(es, nc, 'tmask', [128, NT], F32)
        ntau = T(es, nc, 'ntau', [128, NT], F32)
        mcol = T(es, nc, 'mcol', [128, 1], F32)
        gcol = T(es, nc, 'gcol', [128, 64], F32)

        def ld(dst_t, dst_ap, src_ap, rb=(), eng='sp', slow=False):
            if slow:
                return op(eng, lambda e: e.dma_start(out=dst_ap, in_=src_ap, allow_slow_non_contiguous=True), r=rb, w=(dst_t.b,), dma=dst_t.b)
            return op(eng, lambda e: e.dma_start(out=dst_ap, in_=src_ap), r=rb, w=(dst_t.b,), dma=dst_t.b)

        def st(src_t, src_ap, dst_ap, wb=(), eng='act'):
            return op(eng, lambda e: e.dma_start(out=dst_ap, in_=src_ap), r=(src_t.b,), w=wb, dma=src_t.b)

        for l_ in range(DEPTH):
            for n_ in ('w_in', 'w_out', 'ffn_w_up', 'ffn_w_down'):
                bl = wbf_b[n_][l_]
                rows_ = wbf[n_][l_].shape[0]
                step = rows_ // len(bl)
                for pi_, b_ in enumerate(bl):
                    op('pool', lambda e, n_=n_, l_=l_, pi_=pi_, step=step: e.dma_start(out=wbf[n_][l_][pi_ * step:(pi_ + 1) * step, :],
                                                                                       in_=Wd[n_][l_][pi_ * step:(pi_ + 1) * step, :]),
                       r=(), w=(b_,), dma=b_)
        ld(ident, ident.t[:], c_ident)
        ld(ones, ones.t[:], c_ones)
        op('dve', lambda e: e.tensor_copy(out=identb.t[:], in_=ident.t[:]), r=(ident.b,), w=(identb.b,))
        ld(kmask, kmask.t[:], c_kmask)
        ld(tmask, tmask.t[:], c_tmask)
        ld(ntau, ntau.t[:], c_ntau)
        ld(mcol, mcol.t[:], c_mcol)
        op('dve', lambda e: e.memset(cst.t[:, 0:1], EPS), w=(cst.b,))
        op('dve', lambda e: e.memset(cst.t[:, 3:4], 0.0), w=(cst.b,))
        SHG = math.sqrt(128.0) * 0.5
        SHM = math.sqrt(192.0) * 0.5
        op('dve', lambda e: e.tensor_scalar(out=kbG.t[:], in0=kmask.t[:], scalar1=-SHG, scalar2=None, op0=ALU.add),
           r=(kmask.b,), w=(kbG.b,))
        op('dve', lambda e: e.tensor_scalar(out=kbM.t[:], in0=kmask.t[:], scalar1=-SHM, scalar2=None, op0=ALU.add),
           r=(kmask.b,), w=(kbM.b,))

        psrr = [0]
        ps_allowed = [list(range(8))]

        def psget():
            al = ps_allowed[0]
            i = al[psrr[0] % len(al)]
            psrr[0] += 1
            return PS[i], PSB[i]

        with ExitStack() as ph:
            ccol = T(ph, nc, 'ccol', [128, KC], F32)
            brow = T(ph, nc, 'brow', [1, 6 * D], F32)
            mrow = T(ph, nc, 'mrow', [1, 6 * D], F32)
            wb = [T(ph, nc, 'adaw%d' % i, [128, KC, 512], F32) for i in range(2)]
            ld(ccol, ccol.t[:], c_in.rearrange("(k p) -> p k", p=128), slow=True)
            op('act', lambda e: e.activation(out=ccol.t[:], in_=ccol.t[:], func=AF.Silu), r=(), w=(ccol.b,))
            it = 0
            for l in range(DEPTH):
                ld(brow, brow.t[:], Wd['ada_b'][l].rearrange("(o n) -> o n", o=1))
                for j in range(24):
                    wt = wb[it % 2]; it += 1
                    ld(wt, wt.t[:], Wd['ada_w'][l][:, j * 512:(j + 1) * 512].rearrange("(k p) n -> p k n", p=128))
                    ps, psb = psget()

                    def f(e, ps=ps, wt=wt):
                        for k in range(KC):
                            ins = e.matmul(ps[0:1, :], ccol.t[:, k:k + 1], wt.t[:, k, :], start=(k == 0), stop=(k == KC - 1))
                        return ins
                    op('pe', f, r=(ccol.b, wt.b), w=(psb,))
                    op('dve', lambda e, ps=ps, j=j: e.tensor_tensor(out=mrow.t[:, j * 512:(j + 1) * 512], in0=ps[0:1, :],
                                                                     in1=brow.t[:, j * 512:(j + 1) * 512], op=ALU.add),
                       r=(psb, brow.b), w=(mrow.b,))
                st(mrow, mrow.t[:], mod_scr[l].rearrange("(o n) -> o n", o=1), wb=(mod_b,))
            P.barrier()

        def colvec(dst_col, src_ap, n, eng='sp', extra_r=()):
            k = (n + 127) // 128
            p = min(n, 128)
            return ld(gcol, gcol.t[0:p, dst_col:dst_col + k], src_ap.rearrange("(k p) -> p k", p=p), rb=extra_r, slow=True)

        def norm_phase(ph, l, which, hT, col_of_tile, src_is_x):
            base = 0 if which == 0 else 3
            Abc = T(ph, nc, 'nAbc', [128, D], F32)
            Bbc = T(ph, nc, 'nBbc', [128, D], F32)
            ld(Abc, Abc.t[:], mod_scr[l][(base + 1) * D:(base + 2) * D].partition_broadcast(128), rb=(mod_b,))
            ld(Bbc, Bbc.t[:], Wd['norm1_g' if which == 0 else 'norm2_g'][l].partition_broadcast(128))
            op('dve', lambda e: e.scalar_tensor_tensor(out=Abc.t[:], in0=Abc.t[:], scalar=1.0, in1=Bbc.t[:], op0=ALU.add, op1=ALU.mult),
               r=(Bbc.b,), w=(Abc.b,))
            ld(Bbc, Bbc.t[:], mod_scr[l][base * D:(base + 1) * D].partition_broadcast(128), rb=(mod_b,))
            xt = [T(ph, nc, 'nx%d' % i, [128, D], F32) for i in range(3)]
            xb = [T(ph, nc, 'nxb%d' % i, [128, D], BF16) for i in range(3)]
            junk = T(ph, nc, 'njunk', [128, D], BF16)
            ss = [T(ph, nc, 'nss%d' % i, [128, 2], F32) for i in range(3)]
            def tile_gen(i, tt, dc, sc, ncl):
                x = xt[i % 3]; s_ = ss[i % 3]; xb_ = xb[i % 3]
                src = x_in if src_is_x else y_out
                ld(x, x.t[:], src[tt * 128:(tt + 1) * 128, :], rb=(res_b[tt],))
                op('dve', lambda e, s_=s_: e.memset(s_.t[:], 0.0), w=(s_.b,))
                op('act', lambda e, x=x, s_=s_: e.activation(out=junk.t[:], in_=x.t[:], func=AF.Square, accum_out=s_.t[:, 0:1]),
                   r=(x.b,), w=(junk.b, s_.b))
                op('act', lambda e, s_=s_: e.activation(out=s_.t[:, 1:2], in_=s_.t[:, 0:1], func=AF.Sqrt, scale=1.0 / D,
                                                        bias=cst.t[:, 0:1]), r=(cst.b,), w=(s_.b,))
                yield
                op('dve', lambda e, s_=s_: e.reciprocal(out=s_.t[:, 1:2], in_=s_.t[:, 1:2]), w=(s_.b,))
                op('dve', lambda e, x=x, s_=s_: e.scalar_tensor_tensor(out=x.t[:], in0=x.t[:], scalar=s_.t[:, 1:2], in1=Abc.t[:],
                                                                         op0=ALU.mult, op1=ALU.mult), r=(s_.b, Abc.b), w=(x.b,))
                op('dve', lambda e, x=x, xb_=xb_: e.tensor_tensor(out=xb_.t[:], in0=x.t[:], in1=Bbc.t[:], op=ALU.add), r=(x.b, Bbc.b), w=(xb_.b,))
                yield
                for q in range(2):
                    ps, psb = psget()
                    psv = ps[:, :].bitcast(BF16)

                    def f(e, psv=psv, xb_=xb_, q=q):
                        for j in range(8):
                            k = q * 8 + j
                            ins = e.transpose(psv[:, j * 128:(j + 1) * 128], xb_.t[:, k * 128:(k + 1) * 128], identb.t[:])
                        return ins
                    op('pe', f, r=(xb_.b, identb.b), w=(psb,))
                    src_ap = psv.rearrange("p (j t) -> p j t", t=128)[:, :, sc:sc + ncl]
                    dst_ap = hT.t[:, q * 8:(q + 1) * 8, dc:dc + ncl]
                    if q == 0:
                        op('act', lambda e, src_ap=src_ap, dst_ap=dst_ap: e.copy(out=dst_ap, in_=src_ap), r=(psb,), w=(hT.b,))
                    else:
                        op('dve', lambda e, src_ap=src_ap, dst_ap=dst_ap: e.tensor_copy(out=dst_ap, in_=src_ap), r=(psb,), w=(hT.b,))
            run_pipelined((tile_gen(i, *c) for i, c in enumerate(col_of_tile)), 3)

        def fm_sumsq(parts, n, ps, psb, sq_tiles):
            for i, (t_, ap) in enumerate(parts):
                sq = sq_tiles[i % len(sq_tiles)]
                p = ap.shape[0]
                op('act', lambda e, sq=sq, ap=ap, p=p: e.activation(out=sq.t[0:p, 0:n], in_=ap, func=AF.Square),
                   r=(t_.b,), w=(sq.b,))
            def f(e):
                for i, (t_, ap) in enumerate(parts):
                    p = ap.shape[0]
                    ins = e.matmul(ps[:, 0:n], ones.t[0:p, :], sq_tiles[i].t[0:p, 0:n], start=(i == 0), stop=(i == len(parts) - 1))
                return ins
            op('pe', f, r=tuple(sq_tiles[i].b for i in range(len(parts))) + (ones.b,), w=(psb,))

        def rstd_from(ps, psb, n, dim, out_t):
            op('act', lambda e: e.activation(out=out_t.t[:, 0:n], in_=ps[:, 0:n], func=AF.Sqrt, scale=1.0 / dim, bias=cst.t[:, 0:1]),
               r=(psb, cst.b), w=(out_t.b,))
            op('dve', lambda e: e.reciprocal(out=out_t.t[:, 0:n], in_=out_t.t[:, 0:n]), w=(out_t.b,))

        for l in range(DEPTH):
            first = (l == 0)

            with ExitStack() as ph:
                hT = T(ph, nc, 'hT', [128, KC, S], BF16)
                with ExitStack() as ph2:
                    norm_phase(ph2, l, 0, hT, [(tt, tt * 128, 0, 128) for tt in range(NT)], first)
                    P.barrier()
                wbuf = [T(ph, nc, 'winb%d' % i, [128, KC, 256], BF16) for i in range(2)]
                rows = [T(ph, nc, 'prow%d' % i, [128, S], F32) for i in range(2)]
                for cc in range(35):
                    c0 = cc * 128
                    cw = min(128, IN_COLS - c0)
                    wt = wbuf[(cc // 2) % 2]; row = rows[cc % 2]
                    wo_ = (cc % 2) * 128
                    if cc % 2 == 0:
                        bw = min(256, IN_COLS - c0)
                        ld(wt, wt.t[:, :, 0:bw], wbf['w_in'][l][:, c0:c0 + bw].rearrange("(k p) n -> p k n", p=128), rb=tuple(wbf_b['w_in'][l]))
                    for tb in range(NB):
                        ps, psb = psget()

                        def f(e, ps=ps, wt=wt, tb=tb, cw=cw, wo_=wo_):
                            for k in range(KC):
                                ins = e.matmul(ps[0:cw, 0:TB], wt.t[:, k, wo_:wo_ + cw], hT.t[:, k, tb * TB:(tb + 1) * TB],
                                               start=(k == 0), stop=(k == KC - 1))
                            return ins
                        op('pe', f, r=(wt.b, hT.b), w=(psb,))
                        en = 'act' if tb % 2 == 0 else 'dve'
                        if en == 'act':
                            op('act', lambda e, ps=ps, row=row, tb=tb, cw=cw: e.copy(out=row.t[0:cw, tb * TB:(tb + 1) * TB], in_=ps[0:cw, 0:TB]),
                               r=(psb,), w=(row.b,))
                        else:
                            op('dve', lambda e, ps=ps, row=row, tb=tb, cw=cw: e.tensor_copy(out=row.t[0:cw, tb * TB:(tb + 1) * TB], in_=ps[0:cw, 0:TB]),
                               r=(psb,), w=(row.b,))
                    st(row, row.t[0:cw, :], proj_scr[c0:c0 + cw, :], wb=(proj_b[cc],))
                P.barrier()

            with ExitStack() as ph:
                wsT = T(ph, nc, 'wsT', [128, 4, 128], BF16)
                wsl = T(ph, nc, 'wsl', [128, 4, 128], F32)
                bsT = T(ph, nc, 'bsT', [128, 4], F32)
                vgbc = T(ph, nc, 'vgbc', [128, 512], F32)
                gnbc = T(ph, nc, 'gnbcA', [128, 512], F32)
                ld(gnbc, gnbc.t[:], Wd['group_norm_g'][l][0:512].partition_broadcast(128))
                ld(wsl, wsl.t[:], Wd['gm_spatial_w'][l].rearrange("h p q -> p h q"))
                ld(bsT, bsT.t[:], Wd['gm_spatial_b'][l].rearrange("h p -> p h"), slow=True)
                ld(vgbc, vgbc.t[:], Wd['gm_vnorm_g'][l].partition_broadcast(128))
                ps, psb = psget()

                def f(e, ps=ps):
                    for h in range(4):
                        ins = e.transpose(ps[:, h * 128:(h + 1) * 128], wsl.t[:, h, :], ident.t[:])
                    return ins
                op('pe', f, r=(wsl.b, ident.b), w=(psb,))
                op('dve', lambda e, ps=ps: e.tensor_copy(out=wsT.t[:].rearrange("p h q -> p (h q)"), in_=ps[:, :]), r=(psb,), w=(wsT.b,))
                uv = [T(ph, nc, 'uv%d' % i, [128, 8, 128], F32) for i in range(2)]
                u_sb2 = [T(ph, nc, 'u_sb%d' % i, [128, 512], F32) for i in range(2)]
                v_sb2 = [T(ph, nc, 'v_sb%d' % i, [128, 512], F32) for i in range(2)]
                vn2 = [T(ph, nc, 'vn%d' % i, [128, 512], BF16) for i in range(2)]
                ya2 = [T(ph, nc, 'ya%d' % i, [128, 512], F32) for i in range(2)]
                yst = [T(ph, nc, 'yst%d' % i, [128, 4, 128], BF16) for i in range(2)]
                sm2 = [T(ph, nc, 'sm%d' % i, [128, 8], F32) for i in range(2)]
                junk2 = [T(ph, nc, 'ajunk%d' % i, [128, 512], BF16) for i in range(2)]
                def a_gen(tt):
                    u_sb = u_sb2[tt % 2]; v_sb = v_sb2[tt % 2]; vn = vn2[tt % 2]; ya = ya2[tt % 2]; sm = sm2[tt % 2]; junk = junk2[tt % 2]
                    t_ = uv[tt % 2]
                    ld(t_, t_.t[:], proj_scr[0:1024, tt * 128:(tt + 1) * 128].rearrange("(c p) t -> p c t", p=128), rb=tuple(proj_b[0:8]))
                    op('act', lambda e, t_=t_: e.activation(out=t_.t[:], in_=t_.t[:], func=AF.Gelu), w=(t_.b,))
                    yield
                    pu, pub = psget()
                    pv, pvb = psget()

                    def f(e, t_=t_, pu=pu, pv=pv):
                        for j in range(4):
                            e.transpose(pu[:, j * 128:(j + 1) * 128], t_.t[:, j, :], ident.t[:])
                        for j in range(4):
                            ins = e.transpose(pv[:, j * 128:(j + 1) * 128], t_.t[:, 4 + j, :], ident.t[:])
                        return ins
                    op('pe', f, r=(t_.b, ident.b), w=(pub, pvb))
                    op('dve', lambda e, pu=pu: e.tensor_copy(out=u_sb.t[:], in_=pu[:, :]), r=(pub,), w=(u_sb.b,))
                    op('dve', lambda e: e.memset(sm.t[:], 0.0), w=(sm.b,))
                    op('act', lambda e, pv=pv: e.activation(out=v_sb.t[:], in_=pv[:, :], func=AF.Identity, accum_out=sm.t[:, 0:1]),
                       r=(pvb,), w=(v_sb.b, sm.b))
                    yield
                    op('dve', lambda e: e.tensor_scalar(out=sm.t[:, 1:2], in0=sm.t[:, 0:1], scalar1=-1.0 / 512, scalar2=None, op0=ALU.mult), w=(sm.b,))
                    op('dve', lambda e: e.tensor_scalar(out=v_sb.t[:], in0=v_sb.t[:], scalar1=sm.t[:, 1:2], scalar2=None, op0=ALU.add),
                       r=(sm.b,), w=(v_sb.b,))
                    op('act', lambda e: e.activation(out=junk.t[:], in_=v_sb.t[:], func=AF.Square, accum_out=sm.t[:, 2:3]),
                       r=(v_sb.b,), w=(junk.b, sm.b))
                    op('act', lambda e: e.activation(out=sm.t[:, 3:4], in_=sm.t[:, 2:3], func=AF.Sqrt, scale=1.0 / 512, bias=cst.t[:, 0:1]),
                       r=(cst.b,), w=(sm.b,))
                    yield
                    op('dve', lambda e: e.reciprocal(out=sm.t[:, 3:4], in_=sm.t[:, 3:4]), w=(sm.b,))
                    op('dve', lambda e: e.scalar_tensor_tensor(out=vn.t[:], in0=v_sb.t[:], scalar=sm.t[:, 3:4], in1=vgbc.t[:],
                                                               op0=ALU.mult, op1=ALU.mult), r=(v_sb.b, sm.b, vgbc.b), w=(vn.b,))
                    yield
                    pm, pmb = psget()

                    def f(e, pm=pm):
                        for h in range(4):
                            ins = e.matmul(pm[:, h * 128:(h + 1) * 128], wsT.t[:, h, :], vn.t[:, h * 128:(h + 1) * 128], start=True, stop=True)
                        return ins
                    op('pe', f, r=(wsT.b, vn.b), w=(pmb,))
                    for h in range(4):
                        op('dve', lambda e, pm=pm, h=h: e.scalar_tensor_tensor(
                            out=ya.t[:, h * 128:(h + 1) * 128], in0=pm[:, h * 128:(h + 1) * 128], scalar=bsT.t[:, h:h + 1],
                            in1=u_sb.t[:, h * 128:(h + 1) * 128], op0=ALU.add, op1=ALU.mult), r=(pmb, bsT.b, u_sb.b), w=(ya.b,))
                    yield
                    op('act', lambda e: e.activation(out=junk.t[:], in_=ya.t[:], func=AF.Square, accum_out=sm.t[:, 4:5]),
                       r=(ya.b,), w=(junk.b, sm.b))
                    op('act', lambda e: e.activation(out=sm.t[:, 5:6], in_=sm.t[:, 4:5], func=AF.Sqrt, scale=1.0 / 512, bias=cst.t[:, 0:1]),
                       r=(cst.b,), w=(sm.b,))
                    yield
                    op('dve', lambda e: e.reciprocal(out=sm.t[:, 5:6], in_=sm.t[:, 5:6]), w=(sm.b,))
                    op('dve', lambda e: e.scalar_tensor_tensor(out=ya.t[:], in0=ya.t[:], scalar=sm.t[:, 5:6], in1=gnbc.t[:, 0:512],
                                                               op0=ALU.mult, op1=ALU.mult), r=(sm.b, gnbc.b), w=(ya.b,))
                    yield
                    pt, ptb = psget()

                    def f(e, pt=pt):
                        for j in range(4):
                            ins = e.transpose(pt[:, j * 128:(j + 1) * 128], ya.t[:, j * 128:(j + 1) * 128], ident.t[:])
                        return ins
                    op('pe', f, r=(ya.b, ident.b), w=(ptb,))
                    ys = yst[tt % 2]
                    op('act', lambda e, pt=pt, ys=ys: e.copy(out=ys.t[:].rearrange("p c t -> p (c t)"), in_=pt[:, :]), r=(ptb,), w=(ys.b,))
                    st(ys, ys.t[:], ycat_scr[0:512, tt * 128:(tt + 1) * 128].rearrange("(c p) t -> p c t", p=128), wb=(ycat_b,))
                run_pipelined((a_gen(tt) for tt in range(NT)), 2)
                P.barrier()

            with ExitStack() as ph:
                ft = T(ph, nc, 'feats', [33, S], F32)
                w1 = T(ph, nc, 'hw1', [33, 64], F32)
                w2 = T(ph, nc, 'hw2', [64, 64], F32)
                w3 = T(ph, nc, 'hw3', [64, 2048], F32)
                cv = T(ph, nc, 'hcv', [64, 8], F32)
                h1 = T(ph, nc, 'h1T', [64, S], F32)
                h2 = T(ph, nc, 'h2T', [64, S], F32)
                sq = T(ph, nc, 'hsq', [64, TB], F32)
                adec = T(ph, nc, 'adec', [128, 2048], F32)
                warg = T(ph, nc, 'warg', [128, 2048], F32)
                hful = T(ph, nc, 'hful', [128, 2048], F32)
                fst = [T(ph, nc, 'fst%d' % i, [128, 4, 512], BF16) for i in range(2)]
                ld(ft, ft.t[:], c_feats)
                ld(w1, w1.t[:], Wd['hy_w1'][l])
                ld(w2, w2.t[:], Wd['hy_w2'][l])
                ld(w3, w3.t[:], Wd['hy_w3'][l])
                for i, n in enumerate(['hy_b1', 'hy_f1', 'hy_b2', 'hy_f2']):
                    ld(cv, cv.t[:, i:i + 1], Wd[n][l].rearrange("(p o) -> p o", o=1), slow=True)
                ld(adec, adec.t[:], Wd['hy_decay'][l].partition_broadcast(128))
                op('act', lambda e: e.activation(out=adec.t[:], in_=adec.t[:], func=AF.Abs), w=(adec.b,))
                for i in range(2):
                    op('dve', lambda e, i=i: e.tensor_scalar(out=cv.t[:, 4 + 2 * i:5 + 2 * i], in0=cv.t[:, 1 + 2 * i:2 + 2 * i], scalar1=1.0 / 3,
                                                             scalar2=None, op0=ALU.mult), w=(cv.b,))
                    op('dve', lambda e, i=i: e.tensor_tensor(out=cv.t[:, 5 + 2 * i:6 + 2 * i], in0=cv.t[:, 4 + 2 * i:5 + 2 * i],
                                                             in1=cv.t[:, 2 * i:2 * i + 1], op=ALU.mult), w=(cv.b,))
                for li, (wt_, src, dst) in enumerate(((w1, ft, h1), (w2, h1, h2))):
                    for tb in range(NB):
                        ps, psb = psget()
                        op('pe', lambda e, ps=ps, wt_=wt_, src=src, tb=tb: e.matmul(ps[0:64, 0:TB], wt_.t[:, :], src.t[:, tb * TB:(tb + 1) * TB],
                                                                                     start=True, stop=True), r=(wt_.b, src.b), w=(psb,))
                        sl = dst.t[:, tb * TB:(tb + 1) * TB]
                        op('act', lambda e, ps=ps, sl=sl, li=li: e.activation(out=sl, in_=ps[0:64, 0:TB], func=AF.Sin, scale=cv.t[:, 4 + 2 * li:5 + 2 * li],
                                                                              bias=cv.t[:, 5 + 2 * li:6 + 2 * li]), r=(psb, cv.b), w=(dst.b,))
                        op('dve', lambda e, sl=sl: e.tensor_tensor(out=sq.t[:, 0:TB], in0=sl, in1=sl, op=ALU.mult), r=(dst.b,), w=(sq.b,))
                        op('dve', lambda e: e.tensor_scalar(out=sq.t[:, 0:TB], in0=sq.t[:, 0:TB], scalar1=-4.0, scalar2=3.0, op0=ALU.mult, op1=ALU.add), w=(sq.b,))
                        op('dve', lambda e, sl=sl: e.tensor_tensor(out=sl, in0=sl, in1=sq.t[:, 0:TB], op=ALU.mult), r=(sq.b,), w=(dst.b,))
                for tt in range(NT):
                    op('dve', lambda e, tt=tt: e.tensor_scalar(out=warg.t[:], in0=adec.t[:], scalar1=ntau.t[:, tt:tt + 1], scalar2=None, op0=ALU.mult),
                       r=(adec.b, ntau.b), w=(warg.b,))
                    op('act', lambda e: e.activation(out=warg.t[:], in_=warg.t[:], func=AF.Exp), w=(warg.b,))
                    for nb in range(4):
                        ps, psb = psget()
                        op('pe', lambda e, ps=ps, tt=tt, nb=nb: e.matmul(ps[:, :], h2.t[:, tt * 128:(tt + 1) * 128], w3.t[:, nb * 512:(nb + 1) * 512],
                                                                         start=True, stop=True), r=(h2.b, w3.b), w=(psb,))
                        op('dve', lambda e, ps=ps, nb=nb: e.tensor_tensor(out=hful.t[:, nb * 512:(nb + 1) * 512], in0=ps[:, :],
                                                                          in1=warg.t[:, nb * 512:(nb + 1) * 512], op=ALU.mult), r=(psb, warg.b), w=(hful.b,))
                    if tt == 0:
                        for o in range(2):
                            op('dve', lambda e, o=o: e.memset(hful.t[0:1, o * 1024 + 512:o * 1024 + 1024], 0.0), w=(hful.b,))
                    fs = fst[tt % 2]
                    for o in range(2):
                        op('dve', lambda e, o=o, fs=fs: e.tensor_tensor(out=fs.t[:, 2 * o, :], in0=hful.t[:, o * 1024:o * 1024 + 512],
                                                                        in1=hful.t[:, o * 1024 + 512:o * 1024 + 1024], op=ALU.add), r=(hful.b,), w=(fs.b,))
                        op('dve', lambda e, o=o, fs=fs: e.tensor_tensor(out=fs.t[:, 2 * o + 1, :], in0=hful.t[:, o * 1024:o * 1024 + 512],
                                                                        in1=hful.t[:, o * 1024 + 512:o * 1024 + 1024], op=ALU.subtract), r=(hful.b,), w=(fs.b,))
                    st(fs, fs.t[:], filt_scr[:, tt * 128:(tt + 1) * 128, :].rearrange("f p c -> p f c"), wb=(filt_b,))
                P.barrier()
            with ExitStack() as ph:
                cw_ = T(ph, nc, 'hcw', [128, 4, 12], F32)
                for i in range(3):
                    ld(cw_, cw_.t[:, i, :], Wd['hy_conv_w'][l][i].rearrange("(c p) -> p c", p=128), slow=True)
                ld(cw_, cw_.t[:, 3, :], Wd['hy_conv_b'][l].rearrange("(c p) -> p c", p=128), slow=True)
                rin = [T(ph, nc, 'hrin%d' % i, [128, S + 2], F32) for i in range(2)]
                racc2 = [T(ph, nc, 'hracc%d' % i, [128, S], F32) for i in range(2)]
                hst = [T(ph, nc, 'hst%d' % i, [128, 4, 128], F32) for i in range(2)]
                for r_ in rin:
                    op('dve', lambda e, r_=r_: e.memset(r_.t[:, 0:1], 0.0), w=(r_.b,))
                    op('dve', lambda e, r_=r_: e.memset(r_.t[:, S + 1:S + 2], 0.0), w=(r_.b,))
                si = 0
                for ch in range(12):
                    r_ = rin[ch % 2]
                    racc = racc2[ch % 2]
                    ld(r_, r_.t[:, 1:S + 1], proj_scr[1024 + ch * 128:1024 + (ch + 1) * 128, :], rb=(proj_b[8 + ch],))
                    op('dve', lambda e, r_=r_: e.tensor_scalar(out=r_.t[:, Lp + 1:Lp + 2], in0=r_.t[:, Lp + 1:Lp + 2], scalar1=mcol.t[:, 0:1],
                                                               scalar2=None, op0=ALU.mult), r=(mcol.b,), w=(r_.b,))
                    op('dve', lambda e, r_=r_, ch=ch: e.tensor_scalar(out=racc.t[:], in0=r_.t[:, 1:S + 1], scalar1=cw_.t[:, 1, ch:ch + 1],
                                                                      scalar2=cw_.t[:, 3, ch:ch + 1], op0=ALU.mult, op1=ALU.add), r=(r_.b, cw_.b), w=(racc.b,))
                    op('dve', lambda e, r_=r_, ch=ch: e.scalar_tensor_tensor(out=racc.t[:], in0=r_.t[:, 0:S], scalar=cw_.t[:, 0, ch:ch + 1], in1=racc.t[:],
                                                                             op0=ALU.mult, op1=ALU.add), r=(r_.b, cw_.b), w=(racc.b,))
                    op('dve', lambda e, r_=r_, ch=ch: e.scalar_tensor_tensor(out=racc.t[:], in0=r_.t[:, 2:S + 2], scalar=cw_.t[:, 2, ch:ch + 1], in1=racc.t[:],
                                                                             op0=ALU.mult, op1=ALU.add), r=(r_.b, cw_.b), w=(racc.b,))
                    part = ch // 4
                    cch = ch % 4
                    for tq in range(NT // 4):
                        ps, psb = psget()

                        def f(e, ps=ps, tq=tq):
                            for j in range(4):
                                tt = tq * 4 + j
                                ins = e.transpose(ps[:, j * 128:(j + 1) * 128], racc.t[:, tt * 128:(tt + 1) * 128], ident.t[:])
                            return ins
                        op('pe', f, r=(racc.b, ident.b), w=(psb,))
                        hs = hst[si % 2]; si += 1
                        if part == 2:
                            for j in range(4):
                                tt = tq * 4 + j
                                op('dve', lambda e, ps=ps, hs=hs, j=j, tt=tt: e.tensor_scalar(out=hs.t[:, j, :], in0=ps[:, j * 128:(j + 1) * 128],
                                                                                              scalar1=tmask.t[:, tt:tt + 1], scalar2=None, op0=ALU.mult),
                                   r=(psb, tmask.b), w=(hs.b,))
                        else:
                            op('act', lambda e, ps=ps, hs=hs: e.copy(out=hs.t[:].rearrange("p j c -> p (j c)"), in_=ps[:, :]), r=(psb,), w=(hs.b,))
                        st(hs, hs.t[:], hy_scr[part, tq * 512:(tq + 1) * 512, cch * 128:(cch + 1) * 128].rearrange("(j p) c -> p j c", p=128), wb=(hy_b,))
                P.barrier()
            for o in range(2):
                with ExitStack() as ph:
                    xs = T(ph, nc, 'cx', [128, NT, 512], BF16)
                    ss_ = T(ph, nc, 'cs', [128, NT, 512], BF16)
                    dd_ = T(ph, nc, 'cd', [128, NT, 512], BF16)
                    fre = [T(ph, nc, 'fre%d' % i, [128, NT, 128], BF16) for i in range(2)]
                    fim = [T(ph, nc, 'fim%d' % i, [128, NT, 128], BF16) for i in range(2)]
                    ksb = [T(ph, nc, 'ksb%d' % i, [128, 512], F32) for i in range(2)]
                    tmp = [T(ph, nc, 'ctmp%d' % i, [128, 512], F32) for i in range(2)]
                    zst = [T(ph, nc, 'zst%d' % i, [128, 2, 512], BF16) for i in range(2)]
                    src = hy_scr[2] if o == 0 else y1_scr
                    srcb = hy_b if o == 0 else y1_b
                    xstg = [T(ph, nc, 'xstg%d' % i, [128, 4, 512], F32) for i in range(2)]
                    for q_ in range(NT // 4):
                        sg = xstg[q_ % 2]
                        ld(sg, sg.t[:], src[q_ * 512:(q_ + 1) * 512, :].rearrange("(k p) c -> p k c", p=128), rb=(srcb,))
                        op('dve', lambda e, sg=sg, q_=q_: e.tensor_copy(out=xs.t[:, q_ * 4:(q_ + 1) * 4, :], in_=sg.t[:]), r=(sg.b,), w=(xs.b,))
                    ld(ss_, ss_.t[:], filt_scr[2 * o].rearrange("(k p) c -> p k c", p=128), rb=(filt_b,))
                    ld(dd_, dd_.t[:], filt_scr[2 * o + 1].rearrange("(k p) c -> p k c", p=128), rb=(filt_b,))
                    for i in range(NF + 1):
                        fr = fre[i % 2]; fi = fim[i % 2]; zs = zst[i % 2]
                        ld(fr, fr.t[:], c_Fre[i])
                        pxr, pxrb = psget()
                        pkr, pkrb = psget()

                        def f(e, fr=fr, pxr=pxr, pkr=pkr):
                            for k in range(NT):
                                e.matmul(pxr[:, :], fr.t[:, k, :], xs.t[:, k, :], start=(k == 0), stop=(k == NT - 1))
                            for k in range(NT):
                                ins = e.matmul(pkr[:, :], fr.t[:, k, :], ss_.t[:, k, :], start=(k == 0), stop=(k == NT - 1))
                            return ins
                        op('pe', f, r=(fr.b, xs.b, ss_.b), w=(pxrb, pkrb))
                        op('act', lambda e, pkr=pkr: e.copy(out=ksb[0].t[:], in_=pkr[:, :]), r=(pkrb,), w=(ksb[0].b,))
                        if i < NF:
                            ld(fi, fi.t[:], c_Fim[i])
                            pxi, pxib = psget()
                            pki, pkib = psget()

                            def f(e, fi=fi, pxi=pxi, pki=pki):
                                for k in range(NT):
                                    e.matmul(pxi[:, :], fi.t[:, k, :], xs.t[:, k, :], start=(k == 0), stop=(k == NT - 1))
                                for k in range(NT):
                                    ins = e.matmul(pki[:, :], fi.t[:, k, :], dd_.t[:, k, :], start=(k == 0), stop=(k == NT - 1))
                                return ins
                            op('pe', f, r=(fi.b, xs.b, dd_.b), w=(pxib, pkib))
                            op('act', lambda e, pki=pki: e.copy(out=ksb[1].t[:], in_=pki[:, :]), r=(pkib,), w=(ksb[1].b,))
                            op('dve', lambda e, pxr=pxr: e.tensor_tensor(out=tmp[0].t[:], in0=pxr[:, :], in1=ksb[0].t[:], op=ALU.mult), r=(pxrb, ksb[0].b), w=(tmp[0].b,))
                            op('dve', lambda e, pxi=pxi: e.tensor_tensor(out=tmp[1].t[:], in0=pxi[:, :], in1=ksb[1].t[:], op=ALU.mult), r=(pxib, ksb[1].b), w=(tmp[1].b,))
                            op('dve', lambda e, zs=zs: e.tensor_tensor(out=zs.t[:, 0, :], in0=tmp[0].t[:], in1=tmp[1].t[:], op=ALU.subtract),
                               r=(tmp[0].b, tmp[1].b), w=(zs.b,))
                            op('dve', lambda e, pxr=pxr: e.tensor_tensor(out=tmp[0].t[:], in0=pxr[:, :], in1=ksb[1].t[:], op=ALU.mult), r=(pxrb, ksb[1].b), w=(tmp[0].b,))
                            op('dve', lambda e, pxi=pxi: e.tensor_tensor(out=tmp[1].t[:], in0=pxi[:, :], in1=ksb[0].t[:], op=ALU.mult), r=(pxib, ksb[0].b), w=(tmp[1].b,))
                            op('dve', lambda e, zs=zs: e.tensor_tensor(out=zs.t[:, 1, :], in0=tmp[0].t[:], in1=tmp[1].t[:], op=ALU.add),
                               r=(tmp[0].b, tmp[1].b), w=(zs.b,))
                            st(zs, zs.t[:], z_scr[2 * i:2 * i + 2].rearrange("z p c -> p z c"), wb=(z_b,))
                        else:
                            op('dve', lambda e, pxr=pxr, zs=zs: e.tensor_tensor(out=zs.t[:, 0, :], in0=pxr[:, :], in1=ksb[0].t[:], op=ALU.mult),
                               r=(pxrb, ksb[0].b), w=(zs.b,))
                            st(zs, zs.t[:, 0, :], z_scr[2 * i], wb=(z_b,))
                    P.barrier()
                with ExitStack() as ph:
                    zz = T(ph, nc, 'zz', [128, 2 * NF + 1, 512], BF16)
                    ire = [T(ph, nc, 'ire%d' % i, [128, NF + 1, 128], BF16) for i in range(2)]
                    iim = [T(ph, nc, 'iim%d' % i, [128, NF, 128], BF16) for i in range(2)]
                    skbc = T(ph, nc, 'skbc', [128, 512], F32)
                    gnbc = T(ph, nc, 'gnbcB', [128, 512], F32)
                    ld(gnbc, gnbc.t[:], Wd['group_norm_g'][l][512:1024].partition_broadcast(128))
                    vt = [T(ph, nc, 'cvt%d' % i, [128, 512], F32) for i in range(2)]
                    mt = [T(ph, nc, 'cmt%d' % i, [128, 512], F32) for i in range(2)]
                    yo = [T(ph, nc, 'cyo%d' % i, [128, 512], F32) for i in range(2)]
                    sm = T(ph, nc, 'csm', [128, 4], F32)
                    junk = T(ph, nc, 'cjunk', [128, 512], BF16)
                    yst = [T(ph, nc, 'cyst%d' % i, [128, 4, 128], BF16) for i in range(2)]
                    ld(zz, zz.t[:], z_scr.rearrange("z p c -> p z c"), rb=(z_b,))
                    ld(skbc, skbc.t[:], Wd['hy_skip'][l][o].partition_broadcast(128))
                    src = hy_scr[2] if o == 0 else y1_scr
                    srcb = hy_b if o == 0 else y1_b
                    for tt in range(NT):
                        ir = ire[tt % 2]; ii = iim[tt % 2]; v_ = vt[tt % 2]; m_ = mt[tt % 2]; y_ = yo[tt % 2]
                        ld(ir, ir.t[:], c_Ire[tt])
                        ld(ii, ii.t[:], c_Iim[tt])
                        ld(v_, v_.t[:], src[tt * 128:(tt + 1) * 128, :], rb=(srcb,))
                        ld(m_, m_.t[:], hy_scr[o][tt * 128:(tt + 1) * 128, :], rb=(hy_b,))
                        ps, psb = psget()

                        def f(e, ps=ps, ir=ir, ii=ii):
                            for k in range(NF + 1):
                                e.matmul(ps[:, :], ir.t[:, k, :], zz.t[:, 2 * k, :], start=(k == 0), stop=False)
                            for k in range(NF):
                                ins = e.matmul(ps[:, :], ii.t[:, k, :], zz.t[:, 2 * k + 1, :], start=False, stop=(k == NF - 1))
                            return ins
                        op('pe', f, r=(ir.b, ii.b, zz.b), w=(psb,))
                        op('dve', lambda e, v_=v_: e.tensor_tensor(out=v_.t[:], in0=v_.t[:], in1=skbc.t[:], op=ALU.mult), r=(skbc.b,), w=(v_.b,))
                        op('dve', lambda e, v_=v_, ps=ps: e.tensor_tensor(out=v_.t[:], in0=ps[:, :], in1=v_.t[:], op=ALU.add), r=(psb,), w=(v_.b,))
                        op('dve', lambda e, v_=v_, m_=m_, y_=y_, tt=tt: e.scalar_tensor_tensor(out=y_.t[:], in0=v_.t[:], scalar=tmask.t[:, tt:tt + 1], in1=m_.t[:],
                                                                                             op0=ALU.mult, op1=ALU.mult), r=(v_.b, m_.b, tmask.b), w=(y_.b,))
                        if o == 0:
                            st(y_, y_.t[:], y1_scr[tt * 128:(tt + 1) * 128, :], wb=(y1_b,))
                        else:
                            op('dve', lambda e: e.memset(sm.t[:], 0.0), w=(sm.b,))
                            op('act', lambda e, y_=y_: e.activation(out=junk.t[:], in_=y_.t[:], func=AF.Square, accum_out=sm.t[:, 0:1]),
                               r=(y_.b,), w=(junk.b, sm.b))
                            op('act', lambda e: e.activation(out=sm.t[:, 1:2], in_=sm.t[:, 0:1], func=AF.Sqrt, scale=1.0 / 512, bias=cst.t[:, 0:1]),
                               r=(cst.b,), w=(sm.b,))
                            op('dve', lambda e: e.reciprocal(out=sm.t[:, 1:2], in_=sm.t[:, 1:2]), w=(sm.b,))
                            op('dve', lambda e, y_=y_: e.scalar_tensor_tensor(out=y_.t[:], in0=y_.t[:], scalar=sm.t[:, 1:2], in1=gnbc.t[:, 0:512],
                                                                              op0=ALU.mult, op1=ALU.mult), r=(sm.b, gnbc.b), w=(y_.b,))
                            pt, ptb = psget()

                            def f(e, pt=pt, y_=y_):
                                for j in range(4):
                                    ins = e.transpose(pt[:, j * 128:(j + 1) * 128], y_.t[:, j * 128:(j + 1) * 128], ident.t[:])
                                return ins
                            op('pe', f, r=(y_.b, ident.b), w=(ptb,))
                            ys = yst[tt % 2]
                            op('act', lambda e, pt=pt, ys=ys: e.copy(out=ys.t[:].rearrange("p c t -> p (c t)"), in_=pt[:, :]), r=(ptb,), w=(ys.b,))
                            st(ys, ys.t[:], ycat_scr[512:1024, tt * 128:(tt + 1) * 128].rearrange("(c p) t -> p c t", p=128), wb=(ycat_b,))
                    P.barrier()

            def attention(ph, head_fn, scale, kb, grp, tagp):
                pbuf = [T(ph, nc, tagp + 'pb%d' % i, [128, TB], BF16) for i in range(3)]
                rd = T(ph, nc, tagp + 'rd', [128, TB], F32)
                yh = [T(ph, nc, tagp + 'yh%d' % i, [128, TB], F32) for i in range(4)]
                sqs = [T(ph, nc, tagp + 'sq%d' % i, [128, TB], BF16) for i in range(4)]
                rs = T(ph, nc, tagp + 'rs', [128, TB], F32)
                yst = [T(ph, nc, tagp + 'yst%d' % i, [128, TB], BF16) for i in range(2)]
                gc = 52
                colvec(gc, Wd['group_norm_g'][l][grp * 512:(grp + 1) * 512], 512)
                pi = 0
                si = 0
                hc = 0
                LA = 2
                for qb in range(NB):
                    ps_allowed[0] = [3]
                    heads = head_fn(qb)
                    steps = [(hi, kt) for hi in range(len(heads)) for kt in range(NT)]
                    banks = {}
                    for hi in range(len(heads)):
                        ob = 4 + 2 * (hc % 2); hc += 1
                        banks[hi] = (PS[ob], PSB[ob], PS[ob + 1], PSB[ob + 1])
                    slots = {}

                    def qk(hi, kt):
                        nonlocal pi
                        hd = heads[hi]
                        ps, psb = PS[pi % 3], PSB[pi % 3]
                        pb = pbuf[pi % 3]; pi += 1
                        slots[(hi, kt)] = pb

                        def f(e, ps=ps, hd=hd, kt=kt):
                            n = len(hd['q'])
                            for i in range(n):
                                ins = e.matmul(ps[:, 0:TB], hd['k'][i][1](kt), hd['q'][i][1], start=(i == 0), stop=(i == n - 1))
                            return ins
                        op('pe', f, r=tuple(t_.b for t_, _ in hd['q']) + tuple(t_.b for t_, _ in hd['k']), w=(psb,))
                        op('act', lambda e, ps=ps, pb=pb, kt=kt: e.activation(out=pb.t[:, :], in_=ps[:, 0:TB], func=AF.Exp, scale=scale, bias=kb.t[:, kt:kt + 1]),
                           r=(psb, kb.b), w=(pb.b,))

                    for i in range(min(LA, len(steps))):
                        qk(*steps[i])
                    for i, (hi, kt) in enumerate(steps):
                        if i + LA < len(steps):
                            qk(*steps[i + LA])
                        hd = heads[hi]
                        po, pob, pd, pdb = banks[hi]
                        pb = slots.pop((hi, kt))

                        def f(e, pb=pb, hd=hd, kt=kt, po=po, pd=pd):
                            e.matmul(po[:, 0:TB], hd['v'][1](kt), pb.t[:, :], start=(kt == 0), stop=(kt == NT - 1))
                            return e.matmul(pd[:, 0:TB], ones.t[:, :], pb.t[:, :], start=(kt == 0), stop=(kt == NT - 1))
                        op('pe', f, r=(pb.b, hd['v'][0].b, ones.b), w=(pob, pdb))
                        if kt == NT - 1:
                            op('dve', lambda e, pd=pd: e.reciprocal(out=rd.t[:, :], in_=pd[:, 0:TB]), r=(pdb,), w=(rd.b,))
                            op('dve', lambda e, hi=hi, po=po: e.tensor_tensor(out=yh[hi].t[:, :], in0=po[:, 0:TB], in1=rd.t[:, :], op=ALU.mult), r=(pob, rd.b), w=(yh[hi].b,))
                    pn, pnb = PS[3], PSB[3]
                    fm_sumsq([(yh[i], yh[i].t[:, :]) for i in range(4)], TB, pn, pnb, sqs)
                    rstd_from(pn, pnb, TB, 512, rs)
                    for hi in range(4):
                        ys = yst[si % 2]; si += 1
                        op('dve', lambda e, hi=hi, ys=ys: e.scalar_tensor_tensor(out=ys.t[:, :], in0=yh[hi].t[:, :], scalar=gcol.t[:, gc + hi:gc + hi + 1], in1=rs.t[:, :],
                                                                                 op0=ALU.mult, op1=ALU.mult), r=(yh[hi].b, gcol.b, rs.b), w=(ys.b,))
                        st(ys, ys.t[:, :], ycat_scr[grp * 512 + hi * 128:grp * 512 + (hi + 1) * 128, qb * TB:(qb + 1) * TB], wb=(ycat_b,))
                ps_allowed[0] = list(range(8))

            def make_tabs(ph, tag, p, c_cos, c_sin):
                ct = [T(ph, nc, tag + 'ct%d' % i, [p, TB], F32) for i in range(2)]
                sn = [T(ph, nc, tag + 'sn%d' % i, [p, TB], F32) for i in range(2)]
                cnt = [0]

                def get(tb):
                    i = cnt[0] % 2
                    cnt[0] += 1
                    ld(ct[i], ct[i].t[:, :], c_cos[:, tb * TB:(tb + 1) * TB])
                    ld(sn[i], sn[i].t[:, :], c_sin[:, tb * TB:(tb + 1) * TB])
                    return ct[i], sn[i]
                return get

            def qk_prep(ph_tiles, src_parts, n, gcols, dim, dst_parts, rope):
                sqs, rs, qn, t1, t2 = ph_tiles
                pn, pnb = psget()
                fm_sumsq(src_parts, n, pn, pnb, sqs)
                yield
                rstd_from(pn, pnb, n, dim, rs)
                yield
                for i, (t_, ap) in enumerate(src_parts):
                    p = ap.shape[0]
                    dt_, dap = dst_parts[i]
                    if rope is not None and rope[3] == i:
                        op('dve', lambda e, ap=ap, p=p, i=i: e.scalar_tensor_tensor(out=qn.t[0:p, 0:n], in0=ap, scalar=gcol.t[0:p, gcols[i]:gcols[i] + 1], in1=rs.t[0:p, 0:n],
                                                                                    op0=ALU.mult, op1=ALU.mult), r=(t_.b, gcol.b, rs.b), w=(qn.b,))
                        pr, prb = psget()
                        op('pe', lambda e, pr=pr, p=p: e.matmul(pr[0:p, 0:n], rope[0].t[0:p, 0:p], qn.t[0:p, 0:n], start=True, stop=True), r=(rope[0].b, qn.b), w=(prb,))
                        op('dve', lambda e, p=p: e.tensor_tensor(out=t1.t[0:p, 0:n], in0=qn.t[0:p, 0:n], in1=rope[1].t[0:p, 0:n], op=ALU.mult), r=(qn.b, rope[1].b), w=(t1.b,))
                        yield
                        op('dve', lambda e, pr=pr, p=p: e.tensor_tensor(out=t2.t[0:p, 0:n], in0=pr[0:p, 0:n], in1=rope[2].t[0:p, 0:n], op=ALU.mult), r=(prb, rope[2].b), w=(t2.b,))
                        op('dve', lambda e, p=p, dap=dap: e.tensor_tensor(out=dap, in0=t1.t[0:p, 0:n], in1=t2.t[0:p, 0:n], op=ALU.add), r=(t1.b, t2.b), w=(dt_.b,))
                    else:
                        op('dve', lambda e, ap=ap, p=p, i=i, dap=dap: e.scalar_tensor_tensor(out=dap, in0=ap, scalar=gcol.t[0:p, gcols[i]:gcols[i] + 1], in1=rs.t[0:p, 0:n],
                                                                                             op0=ALU.mult, op1=ALU.mult), r=(t_.b, gcol.b, rs.b), w=(dt_.b,))

            def mk_tiles(ph, tag, nsq):
                return ([T(ph, nc, tag + 'sq%d' % i, [128, TB], BF16) for i in range(nsq)], T(ph, nc, tag + 'rs', [128, TB], F32),
                        T(ph, nc, tag + 'qn', [128, TB], BF16), T(ph, nc, tag + 't1', [128, TB], F32), T(ph, nc, tag + 't2', [128, TB], F32))

            with ExitStack() as ph:
                permG = T(ph, nc, 'permG', [128, 128], BF16)
                ld(permG, permG.t[:], c_permG)
                tabs = make_tabs(ph, 'g', 128, c_cosG, c_sinG)
                colvec(48, Wd['gqa_qn_g'][l], 128)
                colvec(49, Wd['gqa_kn_g'][l], 128)
                QT = [T(ph, nc, 'gQT%d' % i, [128, S], BF16) for i in range(4)]
                KT_ = [T(ph, nc, 'gKT%d' % i, [128, S], BF16) for i in range(2)]
                V_ = [T(ph, nc, 'gV%d' % i, [128, NT, 128], BF16) for i in range(2)]
                rowb = [T(ph, nc, 'grow%d' % i, [128, S], F32) for i in range(2)]
                tsets = [mk_tiles(ph, 'ga', 1), mk_tiles(ph, 'gb', 1)]

                def gq_gen(hh, tb, idx):
                    row = rowb[hh % 2]
                    if tb == 0:
                        ld(row, row.t[:], proj_scr[2560 + hh * 128:2560 + (hh + 1) * 128, :], rb=(proj_b[20 + hh],))
                    dst = QT[hh] if hh < 4 else KT_[hh - 4]
                    sl = slice(tb * TB, (tb + 1) * TB)
                    ct_, sn_ = tabs(tb)
                    yield from qk_prep(tsets[idx % 2], [(row, row.t[:, sl])], TB, [48 if hh < 4 else 49], 128, [(dst, dst.t[:, sl])], (permG, ct_, sn_, 0))
                run_pipelined((gq_gen(hh, tb, hh * NB + tb) for hh in range(6) for tb in range(NB)), 2)
                for j in range(2):
                    row = rowb[j % 2]
                    ld(row, row.t[:], proj_scr[3328 + j * 128:3328 + (j + 1) * 128, :], rb=(proj_b[26 + j],))
                    for tq in range(NT // 4):
                        ps, psb = psget()

                        def f(e, ps=ps, tq=tq, row=row):
                            for jj in range(4):
                                tt = tq * 4 + jj
                                ins = e.transpose(ps[:, jj * 128:(jj + 1) * 128], row.t[:, tt * 128:(tt + 1) * 128], ident.t[:])
                            return ins
                        op('pe', f, r=(row.b, ident.b), w=(psb,))
                        op('act', lambda e, ps=ps, j=j, tq=tq: e.copy(out=V_[j].t[:, tq * 4:(tq + 1) * 4, :].rearrange("p a c -> p (a c)"), in_=ps[:, :]), r=(psb,), w=(V_[j].b,))
                P.barrier()

                def gheads(qb):
                    hs_ = []
                    for h in range(4):
                        j = h // 2
                        hs_.append(dict(q=[(QT[h], QT[h].t[:, qb * TB:(qb + 1) * TB])],
                                        k=[(KT_[j], lambda kt, j=j: KT_[j].t[:, kt * 128:(kt + 1) * 128])],
                                        v=(V_[j], lambda kt, j=j: V_[j].t[:, kt, :])))
                    return hs_
                attention(ph, gheads, 1.0 / math.sqrt(128.0), kbG, 2, 'g')
                P.barrier()

            with ExitStack() as ph:
                permM = T(ph, nc, 'permM', [64, 64], BF16)
                ld(permM, permM.t[:], c_permM)
                tabs = make_tabs(ph, 'm', 64, c_cosM, c_sinM)
                colvec(40, Wd['mla_q_a_g'][l], 512)
                colvec(44, Wd['mla_kv_a_g'][l], 256)
                colvec(46, Wd['mla_qn_g'][l][0:128], 128)
                colvec(47, Wd['mla_qn_g'][l][128:192], 64)
                colvec(50, Wd['mla_kn_g'][l][0:128], 128)
                colvec(51, Wd['mla_kn_g'][l][128:192], 64)
                KnT = [T(ph, nc, 'mKn%d' % i, [128, S], BF16) for i in range(4)]
                KrT = [T(ph, nc, 'mKr%d' % i, [64, S], BF16) for i in range(4)]
                V_ = [T(ph, nc, 'mV%d' % i, [128, NT, 128], BF16) for i in range(4)]
                wq = T(ph, nc, 'mwq', [128, 4, 768], BF16)
                tiles = mk_tiles(ph, 'ma', 4)
                tsets = [tiles, mk_tiles(ph, 'mb', 2)]
                with ExitStack() as ph2:
                    ckvT = T(ph2, nc, 'ckvT', [128, 2, S], BF16)
                    krT = T(ph2, nc, 'krT', [64, S], F32)
                    wkv = T(ph2, nc, 'mwkv', [128, 2, 1024], BF16)
                    with ExitStack() as ph3:
                        wqs = T(ph3, nc, 'mwqs', [128, 4, 768], F32)
                        wkvs = T(ph3, nc, 'mwkvs', [128, 2, 1024], F32)
                        ld(wqs, wqs.t[:], Wd['mla_w_q_b'][l].rearrange("(k p) n -> p k n", p=128))
                        ld(wkvs, wkvs.t[:], Wd['mla_w_kv_b'][l].rearrange("(k p) n -> p k n", p=128))
                        op('dve', lambda e: e.tensor_copy(out=wq.t[:], in_=wqs.t[:]), r=(wqs.b,), w=(wq.b,))
                        op('dve', lambda e: e.tensor_copy(out=wkv.t[:], in_=wkvs.t[:]), r=(wkvs.b,), w=(wkv.b,))
                        P.barrier()
                    ld(krT, krT.t[:], proj_scr[4352:4416, :], rb=(proj_b[34],))
                    rq = [T(ph2, nc, 'mrq%d' % i, [128, 2, TB], F32) for i in range(2)]
                    kn_f2 = [T(ph2, nc, 'mknf%d' % i, [128, TB], F32) for i in range(2)]

                    def mk_gen(tb):
                        tl = tsets[tb % 2]
                        kn_f = kn_f2[tb % 2]
                        sl = slice(tb * TB, (tb + 1) * TB)
                        r2 = rq[tb % 2]
                        ld(r2, r2.t[:], proj_scr[4096:4352, sl].rearrange("(c p) t -> p c t", p=128), rb=tuple(proj_b[32:34]))
                        yield from qk_prep(tl, [(r2, r2.t[:, c, :]) for c in range(2)], TB, [44, 45], 256,
                                           [(ckvT, ckvT.t[:, c, sl]) for c in range(2)], None)
                        ct_, sn_ = tabs(tb)
                        for h in range(4):
                            yield
                            ps, psb = psget()
                            op('pe', lambda e, ps=ps, h=h, sl=sl: [e.matmul(ps[:, 0:TB], wkv.t[:, k, h * 256:h * 256 + 128], ckvT.t[:, k, sl], start=(k == 0), stop=(k == 1))
                                                                   for k in range(2)][-1], r=(wkv.b, ckvT.b), w=(psb,))
                            op('act', lambda e, ps=ps: e.copy(out=kn_f.t[:, :], in_=ps[:, 0:TB]), r=(psb,), w=(kn_f.b,))
                            for jj in range(TPB):
                                tt = tb * TPB + jj
                                ps, psb = psget()
                                op('pe', lambda e, ps=ps, h=h, tt=tt: [e.matmul(ps[:, 0:128], ckvT.t[:, k, tt * 128:(tt + 1) * 128], wkv.t[:, k, h * 256 + 128:h * 256 + 256],
                                                                                  start=(k == 0), stop=(k == 1)) for k in range(2)][-1], r=(wkv.b, ckvT.b), w=(psb,))
                                op('act', lambda e, ps=ps, h=h, tt=tt: e.copy(out=V_[h].t[:, tt, :], in_=ps[:, 0:128]), r=(psb,), w=(V_[h].b,))
                            yield from qk_prep(tl, [(kn_f, kn_f.t[:, :]), (krT, krT.t[:, sl])], TB, [50, 51], 192,
                                               [(KnT[h], KnT[h].t[:, sl]), (KrT[h], KrT[h].t[:, sl])], (permM, ct_, sn_, 1))
                    run_pipelined((mk_gen(tb) for tb in range(NB)), 2)
                    P.barrier()
                rq4 = [T(ph, nc, 'mrq4%d' % i, [128, 4, TB], F32) for i in range(2)]
                qan = T(ph, nc, 'mqan', [128, 4, TB], BF16)
                Qn = [T(ph, nc, 'mQn%d' % i, [128, TB], BF16) for i in range(4)]
                Qr = [T(ph, nc, 'mQr%d' % i, [64, TB], BF16) for i in range(4)]
                qf = T(ph, nc, 'mqf', [128, TB], F32)
                qrf = T(ph, nc, 'mqrf', [64, TB], F32)

                def mheads(qb):
                    sl = slice(qb * TB, (qb + 1) * TB)
                    r_ = rq4[qb % 2]
                    ld(r_, r_.t[:], proj_scr[3584:4096, sl].rearrange("(c p) t -> p c t", p=128), rb=tuple(proj_b[28:32]))
                    for _ in qk_prep(tiles, [(r_, r_.t[:, c, :]) for c in range(4)], TB, [40, 41, 42, 43], 512,
                                     [(qan, qan.t[:, c, :]) for c in range(4)], None):
                        pass
                    ct_, sn_ = tabs(qb)
                    hs_ = []
                    for h in range(4):
                        ps, psb = psget()
                        op('pe', lambda e, ps=ps, h=h: [e.matmul(ps[:, 0:TB], wq.t[:, k, h * 192:h * 192 + 128], qan.t[:, k, :], start=(k == 0), stop=(k == 3))
                                                        for k in range(4)][-1], r=(wq.b, qan.b), w=(psb,))
                        op('act', lambda e, ps=ps: e.copy(out=qf.t[:, :], in_=ps[:, 0:TB]), r=(psb,), w=(qf.b,))
                        ps, psb = psget()
                        op('pe', lambda e, ps=ps, h=h: [e.matmul(ps[0:64, 0:TB], wq.t[:, k, h * 192 + 128:h * 192 + 192], qan.t[:, k, :], start=(k == 0), stop=(k == 3))
                                                        for k in range(4)][-1], r=(wq.b, qan.b), w=(psb,))
                        op('act', lambda e, ps=ps: e.copy(out=qrf.t[:, :], in_=ps[0:64, 0:TB]), r=(psb,), w=(qrf.b,))
                        for _ in qk_prep(tiles, [(qf, qf.t[:, :]), (qrf, qrf.t[:, :])], TB, [46, 47], 192,
                                         [(Qn[h], Qn[h].t[:, :]), (Qr[h], Qr[h].t[:, :])], (permM, ct_, sn_, 1)):
                            pass
                        hs_.append(dict(q=[(Qn[h], Qn[h].t[:, :]), (Qr[h], Qr[h].t[:, :])],
                                        k=[(KnT[h], lambda kt, h=h: KnT[h].t[:, kt * 128:(kt + 1) * 128]), (KrT[h], lambda kt, h=h: KrT[h].t[:, kt * 128:(kt + 1) * 128])],
                                        v=(V_[h], lambda kt, h=h: V_[h].t[:, kt, :])))
                    return hs_
                attention(ph, mheads, 1.0 / math.sqrt(192.0), kbM, 3, 'm')
                P.barrier()

            with ExitStack() as ph:
                wo = T(ph, nc, 'wo', [128, KC, D], BF16)
                g1bc = T(ph, nc, 'g1bc', [128, D], F32)
                ld(g1bc, g1bc.t[:], mod_scr[l][2 * D:3 * D].partition_broadcast(128), rb=(mod_b,))
                for k4 in range(4):
                    ld(wo, wo.t[:, k4 * 4:(k4 + 1) * 4, :], wbf['w_out'][l][k4 * 512:(k4 + 1) * 512, :].rearrange("(k p) n -> p k n", p=128), rb=tuple(wbf_b['w_out'][l]))
                yc = [T(ph, nc, 'oyc%d' % i, [128, KC, TB], BF16) for i in range(2)]
                xt = [T(ph, nc, 'oxt%d' % i, [128, D], F32) for i in range(2)]
                tm_ = T(ph, nc, 'otm', [128, 512], F32)
                for tb in range(NB):
                    yt = yc[tb % 2]
                    ld(yt, yt.t[:], ycat_scr[:, tb * TB:(tb + 1) * TB].rearrange("(k p) t -> p k t", p=128), rb=(ycat_b,))
                    for jj in range(TPB):
                        tt = tb * TPB + jj
                        x = xt[tt % 2]
                        ld(x, x.t[:], (x_in if first else y_out)[tt * 128:(tt + 1) * 128, :], rb=(res_b[tt],))
                        for nb in range(4):
                            ps, psb = psget()

                            def f(e, ps=ps, yt=yt, jj=jj, nb=nb):
                                for k in range(KC):
                                    ins = e.matmul(ps[:, :], yt.t[:, k, jj * 128:(jj + 1) * 128], wo.t[:, k, nb * 512:(nb + 1) * 512], start=(k == 0), stop=(k == KC - 1))
                                return ins
                            op('pe', f, r=(yt.b, wo.b), w=(psb,))
                            op('dve', lambda e, ps=ps, nb=nb: e.tensor_tensor(out=tm_.t[:], in0=ps[:, :], in1=g1bc.t[:, nb * 512:(nb + 1) * 512], op=ALU.mult),
                               r=(psb, g1bc.b), w=(tm_.b,))
                            op('dve', lambda e, x=x, nb=nb: e.tensor_tensor(out=x.t[:, nb * 512:(nb + 1) * 512], in0=x.t[:, nb * 512:(nb + 1) * 512], in1=tm_.t[:], op=ALU.add),
                               r=(tm_.b,), w=(x.b,))
                        st(x, x.t[:], y_out[tt * 128:(tt + 1) * 128, :], wb=(res_b[tt],))
                P.barrier()

            for hb in range(2):
                t0 = hb * FC
                with ExitStack() as ph:
                    h2T = T(ph, nc, 'h2T', [128, KC, FC + 2], BF16)
                    op('dve', lambda e: e.memset(h2T.t[:, :, 0:1], 0.0), w=(h2T.b,))
                    op('dve', lambda e: e.memset(h2T.t[:, :, FC + 1:FC + 2], 0.0), w=(h2T.b,))
                    lst = []
                    if t0 > 0:
                        lst.append((t0 // 128 - 1, 0, 127, 1))
                    for tt in range(t0 // 128, (t0 + FC) // 128):
                        lst.append((tt, 1 + (tt * 128 - t0), 0, 128))
                    if t0 + FC < S:
                        lst.append(((t0 + FC) // 128, FC + 1, 0, 1))
                    with ExitStack() as ph2:
                        norm_phase(ph2, l, 1, h2T, lst, False)
                        P.barrier()
                    cw_ = T(ph, nc, 'fcw', [128, 4, 88], F32)
                    for i in range(3):
                        ld(cw_, cw_.t[:, i, :], Wd['ffn_conv_w'][l][i].rearrange("(c p) -> p c", p=128), slow=True)
                    ld(cw_, cw_.t[:, 3, :], Wd['ffn_conv_b'][l].rearrange("(c p) -> p c", p=128), slow=True)
                    wbuf = [T(ph, nc, 'fwb%d' % i, [128, KC, 2, 256], BF16) for i in range(2)]
                    raw2 = [[T(ph, nc, 'fraw%d_%d' % (i, q), [128, FC + 2], F32) for i in range(2)] for q in range(2)]
                    acc2 = [[T(ph, nc, 'facc%d_%d' % (i, q), [128, FC], F32) for i in range(2)] for q in range(2)]
                    gst = [T(ph, nc, 'fgst%d' % i, [128, FC], BF16) for i in range(2)]
                    blocks = [(c, min(512, FC + 2 - c)) for c in range(0, FC + 2, 512)]
                    def stage1(j):
                        wt = wbuf[(j // 2) % 2]
                        raw = raw2[j % 2]
                        wo_ = (j % 2) * 128
                        if j % 2 == 0:
                            ld(wt, wt.t[:, :, 0, :], wbf['ffn_w_up'][l][:, j * 128:(j + 2) * 128].rearrange("(k p) n -> p k n", p=128), rb=tuple(wbf_b['ffn_w_up'][l]))
                            ld(wt, wt.t[:, :, 1, :], wbf['ffn_w_up'][l][:, DFF + j * 128:DFF + (j + 2) * 128].rearrange("(k p) n -> p k n", p=128), rb=tuple(wbf_b['ffn_w_up'][l]))
                        for gi in range(2):
                            for (c0, n) in blocks:
                                ps, psb = psget()

                                def f(e, ps=ps, wt=wt, gi=gi, c0=c0, n=n, wo_=wo_):
                                    for k in range(KC):
                                        ins = e.matmul(ps[:, 0:n], wt.t[:, k, gi, wo_:wo_ + 128], h2T.t[:, k, c0:c0 + n], start=(k == 0), stop=(k == KC - 1))
                                    return ins
                                op('pe', f, r=(wt.b, h2T.b), w=(psb,))
                                op('act', lambda e, ps=ps, gi=gi, c0=c0, n=n, raw=raw: e.copy(out=raw[gi].t[:, c0:c0 + n], in_=ps[:, 0:n]), r=(psb,), w=(raw[gi].b,))

                    def stage2(j):
                        raw = raw2[j % 2]; acc = acc2[j % 2]
                        for gi in range(2):
                            ci = gi * 44 + j
                            if hb == 0:
                                op('dve', lambda e, gi=gi, raw=raw: e.tensor_scalar(out=raw[gi].t[:, FC + 1:FC + 2], in0=raw[gi].t[:, FC + 1:FC + 2], scalar1=mcol.t[:, 0:1],
                                                                                    scalar2=None, op0=ALU.mult), r=(mcol.b,), w=(raw[gi].b,))
                            op('dve', lambda e, gi=gi, ci=ci, raw=raw, acc=acc: e.tensor_scalar(out=acc[gi].t[:], in0=raw[gi].t[:, 1:FC + 1], scalar1=cw_.t[:, 1, ci:ci + 1],
                                                                                                scalar2=cw_.t[:, 3, ci:ci + 1], op0=ALU.mult, op1=ALU.add), r=(raw[gi].b, cw_.b), w=(acc[gi].b,))
                            op('dve', lambda e, gi=gi, ci=ci, raw=raw, acc=acc: e.scalar_tensor_tensor(out=acc[gi].t[:], in0=raw[gi].t[:, 0:FC], scalar=cw_.t[:, 0, ci:ci + 1], in1=acc[gi].t[:],
                                                                                                       op0=ALU.mult, op1=ALU.add), r=(raw[gi].b, cw_.b), w=(acc[gi].b,))
                            op('dve', lambda e, gi=gi, ci=ci, raw=raw, acc=acc: e.scalar_tensor_tensor(out=acc[gi].t[:], in0=raw[gi].t[:, 2:FC + 2], scalar=cw_.t[:, 2, ci:ci + 1], in1=acc[gi].t[:],
                                                                                                       op0=ALU.mult, op1=ALU.add), r=(raw[gi].b, cw_.b), w=(acc[gi].b,))
                        op('act', lambda e, acc=acc: e.activation(out=acc[0].t[:], in_=acc[0].t[:], func=AF.Silu), w=(acc[0].b,))
                        gs_ = gst[j % 2]
                        op('dve', lambda e, gs_=gs_, acc=acc: e.tensor_tensor(out=gs_.t[:], in0=acc[0].t[:], in1=acc[1].t[:], op=ALU.mult), r=(acc[0].b, acc[1].b), w=(gs_.b,))
                        st(gs_, gs_.t[:], g_scr[j * 128:(j + 1) * 128, t0:t0 + FC], wb=(g_b,))

                    for j in range(45):
                        if j < 44:
                            stage1(j)
                        if j >= 1:
                            stage2(j - 1)
                    P.barrier()

            with ExitStack() as ph:
                wd_ = T(ph, nc, 'wdn', [128, 44, 512], BF16)
                g2bc = T(ph, nc, 'g2bc', [128, D], F32)
                ld(g2bc, g2bc.t[:], mod_scr[l][5 * D:6 * D].partition_broadcast(128), rb=(mod_b,))
                gt = [T(ph, nc, 'dgt%d' % i, [128, 44, GB], BF16) for i in range(2)]
                xt = [T(ph, nc, 'dxt%d' % i, [128, 512], F32) for i in range(2)]
                tm_ = T(ph, nc, 'dtm', [128, 512], F32)
                gi_ = 0
                for fb in range(4):
                    for k4 in range(4):
                        ld(wd_, wd_.t[:, k4 * 11:(k4 + 1) * 11, :], wbf['ffn_w_down'][l][k4 * 11 * 128:(k4 + 1) * 11 * 128, fb * 512:(fb + 1) * 512].rearrange("(k p) n -> p k n", p=128), rb=tuple(wbf_b['ffn_w_down'][l]))
                    for tb in range(S // GB):
                        g_ = gt[gi_ % 2]; gi_ += 1
                        ld(g_, g_.t[:], g_scr[:, tb * GB:(tb + 1) * GB].rearrange("(k p) t -> p k t", p=128), rb=(g_b,))
                        for jj in range(GB // 128):
                            tt = tb * (GB // 128) + jj
                            x = xt[tt % 2]
                            ld(x, x.t[:], y_out[tt * 128:(tt + 1) * 128, fb * 512:(fb + 1) * 512], rb=(res_b[tt],))
                            ps, psb = psget()

                            def f(e, ps=ps, g_=g_, jj=jj):
                                for k in range(44):
                                    ins = e.matmul(ps[:, :], g_.t[:, k, jj * 128:(jj + 1) * 128], wd_.t[:, k, :], start=(k == 0), stop=(k == 43))
                                return ins
                            op('pe', f, r=(g_.b, wd_.b), w=(psb,))
                            op('dve', lambda e, ps=ps, fb=fb: e.tensor_tensor(out=tm_.t[:], in0=ps[:, :], in1=g2bc.t[:, fb * 512:(fb + 1) * 512], op=ALU.mult),
                               r=(psb, g2bc.b), w=(tm_.b,))
                            op('dve', lambda e, x=x: e.tensor_tensor(out=x.t[:], in0=x.t[:], in1=tm_.t[:], op=ALU.add), r=(tm_.b,), w=(x.b,))
                            st(x, x.t[:], y_out[tt * 128:(tt + 1) * 128, fb * 512:(fb + 1) * 512], wb=(res_b[tt],))
                P.barrier()
            print('layer', l, 'engine op counts', dict(P.cnt), 'dma sems', len(P.all_ds), max(d[1] for d in P.all_ds))
            P.rotate()
        P.barrier()
    return nc


def _consts(S, L):
    bf = ml_dtypes.bfloat16
    NT = S // 128
    N = 2 * S
    c = {}
    c['k_ident'] = np.eye(128, dtype=np.float32)
    c['k_ones'] = np.ones((128, 128), dtype=bf)

    def perm(n, half):
        m = np.zeros((n, n), np.float32)
        for d in range(n):
            p = d + half if (d % (2 * half)) < half else d - half
            m[p, d] = 1.0
        return m.astype(bf)
    c['k_permG'] = perm(128, 32)
    c['k_permM'] = perm(64, 16)
    t = np.arange(S)
    row = (t // 64).astype(np.float32)
    col = (t % 64).astype(np.float32)

    def ropetab(n):
        sec = n // 2
        half = sec // 2
        inv = (10000.0 ** (-np.arange(0, sec, 2, dtype=np.float32) / sec)).astype(np.float32)
        cs = np.zeros((n, S), np.float32)
        sn = np.zeros((n, S), np.float32)
        for d in range(n):
            pos = row if d < sec else col
            i = (d % sec) % half
            ang = (pos * inv[i]).astype(np.float32)
            cs[d] = np.cos(ang)
            sn[d] = np.sin(ang) * (-1.0 if (d % sec) < half else 1.0)
        return cs, sn
    c['k_cosG'], c['k_sinG'] = ropetab(128)
    c['k_cosM'], c['k_sinM'] = ropetab(64)
    pos = np.arange(L, dtype=np.float32)
    tt = np.linspace(0.0, 1.0, L, dtype=np.float32)
    bands = np.linspace(1e-4, 15, 16, dtype=np.float32)
    ang = (np.float32(2.0 * math.pi / L) * pos[:, None] * bands[None, :]).astype(np.float32)
    feats = np.concatenate([tt[:, None], np.cos(ang), -np.sin(ang)], axis=-1).astype(np.float32)
    fT = np.zeros((33, S), np.float32)
    fT[:, :L] = feats.T
    c['k_featsT'] = fT
    ntau = np.full((S,), -1e4, np.float32)
    ntau[:L] = -tt
    c['k_negtau'] = np.ascontiguousarray(ntau.reshape(NT, 128).T)
    km = np.full((S,), -30000.0, np.float32); km[:L] = 0.0
    c['k_keymask'] = np.ascontiguousarray(km.reshape(NT, 128).T)
    tm = np.zeros((S,), np.float32); tm[:L] = 1.0
    c['k_tokmask'] = np.ascontiguousarray(tm.reshape(NT, 128).T)
    c['k_mcol'] = np.full((128, 1), 1.0 if L == S else 0.0, np.float32)
    tt_ = np.arange(S, dtype=np.int64)
    f_re = np.arange((NT + 1) * 128, dtype=np.int64)
    ph = (tt_[:, None] * f_re[None, :]) % N
    Fre = np.cos(2.0 * np.pi * ph / N)
    Fre[:, S + 1:] = 0.0
    f_im = np.arange(S, dtype=np.int64)
    ph2 = (tt_[:, None] * f_im[None, :]) % N
    Fim = -np.sin(2.0 * np.pi * ph2 / N)
    c['k_Fre'] = np.ascontiguousarray(Fre.reshape(NT, 128, NT + 1, 128).transpose(2, 1, 0, 3)).astype(bf)
    c['k_Fim'] = np.ascontiguousarray(Fim.reshape(NT, 128, NT, 128).transpose(2, 1, 0, 3)).astype(bf)
    wre = np.full((NT + 1) * 128, 2.0 / N); wre[0] = 1.0 / N; wre[S] = 1.0 / N; wre[S + 1:] = 0.0
    Ire = (np.cos(2.0 * np.pi * ph / N) * wre[None, :]).T
    Iim = (-np.sin(2.0 * np.pi * ph2 / N) * (2.0 / N)).T
    c['k_Ire'] = np.ascontiguousarray(Ire.reshape(NT + 1, 128, NT, 128).transpose(2, 1, 0, 3)).astype(bf)
    c['k_Iim'] = np.ascontiguousarray(Iim.reshape(NT, 128, NT, 128).transpose(2, 1, 0, 3)).astype(bf)
    return c


_CACHE = {}


def run(x_list, c_list, L_list, weights, S, DEPTH):
    key = (S, DEPTH)
    if key not in _CACHE:
        _CACHE[key] = build(S, DEPTH)
    nc = _CACHE[key]
    cc = {}
    in_maps = []
    for x, cvec, L in zip(x_list, c_list, L_list):
        if L not in cc:
            cc[L] = _consts(S, L)
        m = dict(cc[L])
        xp = np.zeros((S, D), np.float32)
        xp[:L] = x
        m['x'] = xp
        m['c'] = np.ascontiguousarray(cvec, dtype=np.float32)
        for n in WEIGHT_NAMES:
            m[n] = weights[n]
        in_maps.append(m)
    res = run_bass_kernel_spmd(nc, in_maps, core_ids=list(range(len(in_maps))))
    return [r['y'] for r in res.results]


def kernel(**inputs):
    S = 4096
    weights = {n: np.ascontiguousarray(np.asarray(inputs[n], dtype=np.float32)) for n in WEIGHT_NAMES}
    xp = np.asarray(inputs['x_prompt'], dtype=np.float32)
    xs = np.asarray(inputs['x_sample'], dtype=np.float32)
    cp = np.asarray(inputs['c_prompt'], dtype=np.float32)
    cs = np.asarray(inputs['c_sample'], dtype=np.float32)
    xl = [xp[i] for i in range(4)] + [xs[i] for i in range(4)]
    cl = [cp[i] for i in range(4)] + [cs[i] for i in range(4)]
    Ls = [2048] * 4 + [4096] * 4
    ys = run(xl, cl, Ls, weights, S, 4)
    y_prompt = np.stack([ys[i][:2048] for i in range(4)], axis=0).astype(np.float32)
    y_sample = np.stack([ys[4 + i] for i in range(4)], axis=0).astype(np.float32)
    return (y_prompt, y_sample)
```
